# Optimizing a Trainium2 kernel written in Bass

```python
import jax, jax.numpy as jnp
from jax import lax
import numpy as np

D_MODEL = 2048
BATCH = 8
SEQ = 2048
DEPTH = 2

GRID_W = 64
BLOCK = 128
HEAD_DIM = 128
ATTN_WIDTH = D_MODEL // 2
N_Q_HEADS = ATTN_WIDTH // HEAD_DIM
N_KV_HEADS = N_Q_HEADS // 4
ROPE_THETA = 10000.0
ROPE_PAIRS = HEAD_DIM // 2
ROPE_FREQ_PER_AXIS = ROPE_PAIRS // 2
CONV_CH = D_MODEL // 2
CONV_WIDTH = 31
CONV_PAD = CONV_WIDTH // 2
SG_CH = D_MODEL // 2
SG_GROUP_CH = 128
SG_GROUPS = SG_CH // SG_GROUP_CH
SG_CHUNK = 128
N_BRANCH = 3
D_FF = -(-8 * D_MODEL // (3 * 256)) * 256

Q_COLS = N_Q_HEADS * HEAD_DIM
KV_COLS = N_KV_HEADS * HEAD_DIM
IN_WIDTHS = [Q_COLS, KV_COLS, KV_COLS, 2 * CONV_CH, 2 * SG_CH, N_BRANCH * D_MODEL]
IN_COLS = int(sum(IN_WIDTHS))
IN_SPLITS = [int(s) for s in np.cumsum(IN_WIDTHS)[:-1]]

kernel_name = "hybrid_gqa_conformer_sgu_encoder"


def rms_norm(x, g, eps=1e-6):
    xf = x.astype(jnp.float32)
    y = xf * lax.rsqrt(jnp.mean(xf * xf, axis=-1, keepdims=True) + eps)
    return (y * g.astype(jnp.float32)).astype(x.dtype)


def layer_norm(x, g, b, eps=1e-5):
    xf = x.astype(jnp.float32)
    mu = jnp.mean(xf, axis=-1, keepdims=True)
    xc = xf - mu
    y = xc * lax.rsqrt(jnp.mean(xc * xc, axis=-1, keepdims=True) + eps)
    return (y * g.astype(jnp.float32) + b.astype(jnp.float32)).astype(x.dtype)


def axial_rope(n):
    rows = n // GRID_W
    row = jnp.repeat(jnp.arange(rows, dtype=jnp.float32), GRID_W)
    col = jnp.tile(jnp.arange(GRID_W, dtype=jnp.float32), rows)
    inv = ROPE_THETA ** (-jnp.arange(ROPE_FREQ_PER_AXIS, dtype=jnp.float32) / ROPE_FREQ_PER_AXIS)
    ang = jnp.concatenate([row[:, None] * inv, col[:, None] * inv], axis=-1)
    return jnp.cos(ang), jnp.sin(ang)


def apply_rope(x, cos, sin):
    xf = x.astype(jnp.float32)
    c = cos[None, :, None, :]
    s = sin[None, :, None, :]
    x1, x2 = xf[..., :ROPE_PAIRS], xf[..., ROPE_PAIRS:]
    return jnp.concatenate([x1 * c - x2 * s, x2 * c + x1 * s], axis=-1).astype(x.dtype)


def blocked_gqa(q, k, v):
    B, S = q.shape[0], q.shape[1]
    nb = S // BLOCK
    grp = N_Q_HEADS // N_KV_HEADS
    qb = q.reshape(B, nb, BLOCK, N_KV_HEADS, grp, HEAD_DIM).transpose(1, 0, 2, 3, 4, 5)
    scale = HEAD_DIM ** -0.5

    def one_block(q_blk):
        s = jnp.einsum('bqhgd,bkhd->bhgqk', q_blk, k).astype(jnp.float32) * scale
        p = jax.nn.softmax(s, axis=-1).astype(v.dtype)
        return jnp.einsum('bhgqk,bkhd->bqhgd', p, v)

    o = lax.map(one_block, qb)
    return o.transpose(1, 0, 2, 3, 4, 5).reshape(B, S, N_Q_HEADS * HEAD_DIM)


def conformer_conv(glu_in, w_dw, b_dw, ln_g, ln_b):
    a, g = jnp.split(glu_in, 2, axis=-1)
    z = a * jax.nn.sigmoid(g)
    z = lax.conv_general_dilated(
        z, w_dw, window_strides=(1,), padding=((CONV_PAD, CONV_PAD),),
        dimension_numbers=('NWC', 'WIO', 'NWC'), feature_group_count=CONV_CH) + b_dw
    z = layer_norm(z, ln_g, ln_b)
    return jax.nn.silu(z)


def spatial_gating(uv, ln_g, ln_b, w_s, b_s):
    B, S = uv.shape[0], uv.shape[1]
    nc = S // SG_CHUNK
    u, v = jnp.split(jax.nn.gelu(uv), 2, axis=-1)
    v = layer_norm(v.reshape(B, S, SG_GROUPS, SG_GROUP_CH),
                   ln_g.reshape(SG_GROUPS, SG_GROUP_CH), ln_b.reshape(SG_GROUPS, SG_GROUP_CH))
    v = v.reshape(B, nc, SG_CHUNK, SG_GROUPS, SG_GROUP_CH)
    v = jnp.einsum('gpq,bnqgc->bnpgc', w_s, v) + b_s.T[None, None, :, :, None]
    return u * v.reshape(B, S, SG_CH)


def setup_inputs(seed: int = 0) -> dict:
    key = jax.random.key(seed)
    ks = jax.random.split(key, 24)
    f32 = jnp.float32

    def nrm(k, shape, scale):
        return jax.random.normal(k, shape, f32) * scale

    L = DEPTH
    return {
        "x": nrm(ks[0], (BATCH, SEQ, D_MODEL), 1.0),
        "g_mix": 1.0 + nrm(ks[1], (L, D_MODEL), 0.02),
        "w_in": nrm(ks[2], (L, D_MODEL, IN_COLS), D_MODEL ** -0.5),
        "b_gate": nrm(ks[3], (L, N_BRANCH * D_MODEL), 0.01),
        "q_norm_g": 1.0 + nrm(ks[4], (L, HEAD_DIM), 0.02),
        "k_norm_g": 1.0 + nrm(ks[5], (L, HEAD_DIM), 0.02),
        "w_attn_o": nrm(ks[6], (L, Q_COLS, D_MODEL), Q_COLS ** -0.5),
        "w_dw": nrm(ks[7], (L, CONV_WIDTH, 1, CONV_CH), CONV_WIDTH ** -0.5),
        "b_dw": nrm(ks[8], (L, CONV_CH), 0.01),
        "conv_ln_g": 1.0 + nrm(ks[9], (L, CONV_CH), 0.02),
        "conv_ln_b": nrm(ks[10], (L, CONV_CH), 0.01),
        "w_conv_o": nrm(ks[11], (L, CONV_CH, D_MODEL), CONV_CH ** -0.5),
        "sg_ln_g": 1.0 + nrm(ks[12], (L, SG_CH), 0.02),
        "sg_ln_b": nrm(ks[13], (L, SG_CH), 0.01),
        "w_s": nrm(ks[14], (L, SG_GROUPS, SG_CHUNK, SG_CHUNK), SG_CHUNK ** -0.5),
        "b_s": 1.0 + nrm(ks[15], (L, SG_GROUPS, SG_CHUNK), 0.01),
        "w_sg_o": nrm(ks[16], (L, SG_CH, D_MODEL), SG_CH ** -0.5),
        "w_out": nrm(ks[17], (L, D_MODEL, D_MODEL), D_MODEL ** -0.5),
        "g_ffn": 1.0 + nrm(ks[18], (L, D_MODEL), 0.02),
        "w_ff_gate": nrm(ks[19], (L, D_MODEL, D_FF), D_MODEL ** -0.5),
        "w_ff_up": nrm(ks[20], (L, D_MODEL, D_FF), D_MODEL ** -0.5),
        "w_ff_down": nrm(ks[21], (L, D_FF, D_MODEL), D_FF ** -0.5),
        "g_final": 1.0 + nrm(ks[22], (D_MODEL,), 0.02),
    }


def reference(x, g_mix, w_in, b_gate, q_norm_g, k_norm_g, w_attn_o, w_dw, b_dw,
              conv_ln_g, conv_ln_b, w_conv_o, sg_ln_g, sg_ln_b, w_s, b_s, w_sg_o,
              w_out, g_ffn, w_ff_gate, w_ff_up, w_ff_down, g_final):
    B, S, _ = x.shape
    cos, sin = axial_rope(S)
    for l in range(DEPTH):
        h = rms_norm(x, g_mix[l])
        proj = h @ w_in[l]
        q, k, v, conv_in, sg_in, gate_logits = jnp.split(proj, IN_SPLITS, axis=-1)

        q = rms_norm(q.reshape(B, S, N_Q_HEADS, HEAD_DIM), q_norm_g[l])
        k = rms_norm(k.reshape(B, S, N_KV_HEADS, HEAD_DIM), k_norm_g[l])
        v = v.reshape(B, S, N_KV_HEADS, HEAD_DIM)
        q = apply_rope(q, cos, sin)
        k = apply_rope(k, cos, sin)
        y_attn = blocked_gqa(q, k, v) @ w_attn_o[l]

        y_conv = conformer_conv(conv_in, w_dw[l], b_dw[l], conv_ln_g[l], conv_ln_b[l]) @ w_conv_o[l]

        y_sg = spatial_gating(sg_in, sg_ln_g[l], sg_ln_b[l], w_s[l], b_s[l]) @ w_sg_o[l]

        gates = jax.nn.sigmoid((gate_logits + b_gate[l]).reshape(B, S, N_BRANCH, D_MODEL))
        merged = gates[:, :, 0] * y_attn + gates[:, :, 1] * y_conv + gates[:, :, 2] * y_sg
        x = x + merged @ w_out[l]

        hf = rms_norm(x, g_ffn[l])
        x = x + (jax.nn.silu(hf @ w_ff_gate[l]) * (hf @ w_ff_up[l])) @ w_ff_down[l]
    return rms_norm(x, g_final)
```

```python
import numpy as np
import ml_dtypes
import concourse.bass as bass
import concourse.mybir as mybir
from concourse.bass_utils import run_bass_kernel_spmd

F32 = mybir.dt.float32
BF16 = mybir.dt.bfloat16
AF = mybir.ActivationFunctionType
ALU = mybir.AluOpType
AX = mybir.AxisListType

D = 2048
S = 2048
NB = 8
DEPTH = 2
T = 512
NT = S // T
IN_COLS = 11776
Q0, K0, V0, CA0, CG0, SU0, SV0, G0 = 0, 1024, 1280, 1536, 2560, 3584, 4608, 5632
DFF = 5632
KSUB = 8
NSLOT = 4
PADZ = 15
ZW = S + 2 * PADZ
EPS_RMS = 1e-6
EPS_LN = 1e-5
SM_SCALE = 128 ** -0.5
GELU_C = 0.044715
GELU_S = 1.5957691216057308


class Buf:
    __slots__ = ("name", "last_write", "readers")

    def __init__(self, name=""):
        self.name = name
        self.last_write = None
        self.readers = {}


class Prog:
    ENGS = ("pe", "act", "dve", "pool", "sp")

    def __init__(self, nc, same_eng_sync=True):
        self.nc = nc
        self.same_eng_sync = same_eng_sync
        self.sems = {}
        self.cnt = {}
        for e in ("pe", "act", "dve", "pool"):
            self.sems[e] = nc.alloc_semaphore("s_" + e)
            self.cnt[e] = 0
        self.waited = {e: {} for e in self.ENGS}
        self.q = {e: [] for e in self.ENGS}
        self.n_dma_sem = 0

    def dma_sem(self, name=""):
        self.n_dma_sem += 1
        key = "d%d_%s" % (self.n_dma_sem, name)
        self.sems[key] = self.nc.alloc_semaphore(key)
        self.cnt[key] = 0
        return key

    def _deps(self, eng, reads, writes):
        need = {}

        def add(t):
            if t is None:
                return
            k, v = t
            if need.get(k, 0) < v:
                need[k] = v
        for b in reads:
            add(b.last_write)
        for b in writes:
            add(b.last_write)
            for k, v in b.readers.items():
                add((k, v))
        for k, v in need.items():
            if k == eng and (eng == "pe" or not self.same_eng_sync):
                continue
            if k == eng and v > self.cnt[eng]:
                continue
            self.wait_tok(eng, (k, v))

    def _mark(self, tok, reads, writes):
        k, v = tok
        for b in reads:
            if b.readers.get(k, 0) < v:
                b.readers[k] = v
        for b in writes:
            b.last_write = tok
            b.readers = {}

    def op(self, eng, fn, reads=(), writes=(), signal=True):
        self._deps(eng, reads, writes)
        if signal:
            self.cnt[eng] += 1
            tok = (eng, self.cnt[eng])
        else:
            tok = (eng, self.cnt[eng] + 1)
        self.q[eng].append(("op", fn, signal, eng, 1))
        self._mark(tok, reads, writes)
        return tok

    def dma(self, eng, semkey, fn, reads=(), writes=(), serialize=True):
        self._deps(eng, reads, writes)
        if serialize and self.cnt[semkey] > 0:
            self.wait_tok(eng, (semkey, self.cnt[semkey]))
        self.cnt[semkey] += 16
        tok = (semkey, self.cnt[semkey])
        self.q[eng].append(("op", fn, True, semkey, 16))
        self._mark(tok, reads, writes)
        return tok

    def wait_tok(self, eng, tok):
        k, v = tok
        if self.waited[eng].get(k, 0) < v:
            self.waited[eng][k] = v
            self.q[eng].append(("wait", k, v))

    def emit(self):
        nc = self.nc
        for e in self.ENGS:
            for item in self.q[e]:
                if item[0] == "wait":
                    assert item[2] <= self.cnt[item[1]], ("unreachable wait", e, item)
        with nc.Block() as block:
            def runner(ename):
                def run(engine):
                    for item in self.q[ename]:
                        if item[0] == "wait":
                            engine.wait_ge(self.sems[item[1]], item[2])
                        else:
                            _, fn, signal, semk, inc = item
                            ins = fn(engine)
                            if signal:
                                ins.then_inc(self.sems[semk], inc)
                return run
            block.sync(runner("sp"))
            block.tensor(runner("pe"))
            block.scalar(runner("act"))
            block.vector(runner("dve"))
            block.gpsimd(runner("pool"))


def build(nlayers=DEPTH, ntilesB=NT, dbg=()):
    nc = bass.Bass("TRN2", target_bir_lowering=False)
    P = Prog(nc)
    L = nlayers

    def din(name, shape, dt=F32):
        return nc.dram_tensor(name, list(shape), dt, kind="ExternalInput").ap()

    x_d = din("x", [S, D])
    w_d = {
        "in": din("w_in", [L, D, IN_COLS]),
        "ao": din("w_attn_o", [L, 1024, D]),
        "co": din("w_conv_o", [L, 1024, D]),
        "so": din("w_sg_o", [L, 1024, D]),
        "out": din("w_out", [L, D, D]),
        "fg": din("w_ff_gate", [L, D, DFF]),
        "fu": din("w_ff_up", [L, D, DFF]),
        "fd": din("w_ff_down", [L, DFF, D]),
    }
    WSHAPE = {"in": (D, IN_COLS), "ao": (1024, D), "co": (1024, D), "so": (1024, D), "out": (D, D),
              "fg": (D, DFF), "fu": (D, DFF), "fd": (DFF, D)}
    gmix_d = din("g_mix_rep", [L, 128, D])
    gffn_d = din("g_ffn_rep", [L, 128, D])
    gfin_d = din("g_final_rep", [128, D])
    bgate_d = din("b_gate_T", [L, 128, 48])
    qg_d = din("qg_rep", [L, 128, 128])
    kg_d = din("kg_rep", [L, 128, 128])
    wdw_d = din("wdw_T", [L, 128, 8, 31])
    cvec_d = din("cvec_T", [L, 128, 5, 8])
    wsT_d = din("wsT", [L, 128, 8, 128])
    bs_d = din("bs_rep", [L, 128, 8, 128])
    cos_d = din("cos_t", [128, 16, 64])
    sin_d = din("sin_t", [128, 16, 64])
    ident_d = din("ident_bf", [128, 128], BF16)
    out_d = nc.dram_tensor("out", [S, D], F32, kind="ExternalOutput").ap()
    dbg_d = {}

    wb_d = {}
    wb_buf = {}
    for l in range(L):
        for k, (r, c) in WSHAPE.items():
            wb_d[(l, k)] = nc.dram_tensor("wb_%s%d" % (k, l), [r, c], BF16).ap()
            wb_buf[(l, k)] = Buf("wb_%s%d" % (k, l))
    xs_d = [nc.dram_tensor("xscr%d" % i, [S, D], F32, kind=("ExternalOutput" if "xscr" in dbg else "Internal")).ap()
            for i in range(max(1, L - 1))]
    xs_buf = [[Buf("xscr%d_%d" % (i, t)) for t in range(NT)] for i in range(max(1, L - 1))]
    zT_d = nc.dram_tensor("zT", [1024, ZW], BF16).ap()
    z_buf = [[Buf("z%d_%d" % (t, j)) for j in range(2)] for t in range(NT)]
    zpad_buf = Buf("zpad")

    def sb(name, shape, dt):
        return nc.alloc_sbuf_tensor(name, list(shape), dt)

    xt = sb("xt", [128, 4, D], F32)
    xt_b = [Buf("xt%d" % s) for s in range(4)]
    hT = sb("hT", [128, 16, T], BF16)
    hT_b = Buf("hT")
    kT = sb("kT", [128, 2, S], BF16)
    kT_b = Buf("kT")
    Vt = sb("Vt", [128, 16, 256], BF16)
    Vt_b = Buf("Vt")
    brT = sb("brT", [128, 24, T], BF16)
    br_b = [Buf("br%d" % i) for i in range(24)]
    mgT = sb("mgT", [128, 16, T], BF16)
    mg_b = [Buf("mg%d" % i) for i in range(16)]
    ring = [sb("ring%d" % i, [128, KSUB, 512], BF16) for i in range(NSLOT)]
    ring_b = [Buf("ring%d" % i) for i in range(NSLOT)]
    ring_sem = [P.dma_sem("ring%d" % i) for i in range(NSLOT)]
    stg = sb("stg", [128, 8, T], F32)
    stg_b = [Buf("stg%d" % i) for i in range(8)]
    OV1 = 16896
    ov1 = sb("ov1", [128, OV1 // 2], BF16)
    ob1, ob2, ob3, ob4, ob5 = [Buf("ov1_%d" % i) for i in range(5)]
    G_REP_B, XS_B, JUNK_B, ZW_B, DLO_B, DHI_B = [ob1], [ob2, ob3], [ob4, ob5], [ob1, ob2], [ob3, ob4], [ob5]
    g_rep = ov1[:, 0:4096].bitcast(F32)
    xs_t = ov1[:, 4096:6144]
    junk = ov1[:, 6144:8192]
    zw = ov1[:, 0:8 * 544].rearrange("p (c w) -> p c w", c=8)
    diag = ov1[:, 4352:4352 + 31 * 128].rearrange("p (k j) -> p k j", k=31)
    ov2 = sb("ov2", [128, 6 * 512], F32)
    ov2_b = [Buf("ov2_%d" % i) for i in range(6)]

    def tmp(i):
        return ov2[:, i * 512:(i + 1) * 512]
    pt = [sb("pt%d" % i, [128, 512], BF16) for i in range(4)]
    pt_b = [Buf("pt%d" % i) for i in range(4)]
    gt = [sb("gt%d" % i, [128, 512], F32) for i in range(2)]
    gt_b = [Buf("gt%d" % i) for i in range(2)]
    pr = [sb("pr%d" % i, [128, 512], F32) for i in range(2)]
    pr_b = [Buf("pr%d" % i) for i in range(2)]
    qnb = [sb("qnb%d" % i, [128, 512], BF16) for i in range(2)]
    qnb_b = [Buf("qnb%d" % i) for i in range(2)]
    small = sb("small", [128, 64], F32)
    small_b = Buf("small")
    ident = sb("ident", [128, 128], BF16)
    ones_bf = sb("ones_bf", [128, 128], BF16)
    ones_f = sb("ones_f", [128, 128], F32)
    epsc = sb("epsc", [128, 2], F32)
    cs_t = sb("cs_t", [128, 2, 4, 64], F32)
    cs_b = Buf("cs")
    bgate = sb("bgate", [128, 48], F32)
    qkg = sb("qkg", [128, 2, 128], F32)
    wdw = sb("wdw", [128, 8, 31], F32)
    cvec = sb("cvec", [128, 5, 8], F32)
    wsT_bf = sb("wsT_bf", [128, 8, 128], BF16)
    Bmat = sb("Bmat", [128, 8, 128], F32)
    const_b = Buf("const")
    lconst_b = Buf("lconst")
    csem = P.dma_sem("const")

    psum = [nc.alloc_psum_tensor("ps%d" % i, [128, 512], F32) for i in range(8)]
    psum_b = [Buf("ps%d" % i) for i in range(8)]
    ps_rr = [0]

    held = set()

    def newps(hold=False):
        for _ in range(17):
            i = ps_rr[0] % 8
            ps_rr[0] += 1
            if i not in held:
                break
        else:
            raise RuntimeError("all PSUM banks held")
        if hold:
            held.add(i)
        return psum[i], psum_b[i]

    def release(*banks):
        for b in banks:
            held.discard(psum_b.index(b[1]))

    def tt(eng, out, in0, in1, op, reads, writes):
        P.op(eng, lambda e: e.tensor_tensor(out=out, in0=in0, in1=in1, op=op), reads, writes)

    def ts(eng, out, in0, s1, s2, op0, op1, reads, writes):
        if op1 is None:
            P.op(eng, lambda e: e.tensor_scalar(out=out, in0=in0, scalar1=s1, scalar2=None, op0=op0), reads, writes)
        else:
            P.op(eng, lambda e: e.tensor_scalar(out=out, in0=in0, scalar1=s1, scalar2=s2, op0=op0, op1=op1), reads, writes)

    def stt(eng, out, in0, scalar, in1, op0, op1, reads, writes):
        P.op(eng, lambda e: e.scalar_tensor_tensor(out=out, in0=in0, scalar=scalar, in1=in1, op0=op0, op1=op1),
             reads, writes)

    def act(out, in_, func, reads, writes, bias=None, scale=None, accum_out=None):
        kw = {}
        if bias is not None:
            kw["bias"] = bias
        if scale is not None:
            kw["scale"] = scale
        if accum_out is not None:
            kw["accum_out"] = accum_out
        P.op("act", lambda e: e.activation(out=out, in_=in_, func=func, **kw), reads, writes)

    def rsqrt(out, in_, scale, eps_col, reads, writes):
        act(out, in_, AF.Sqrt, list(reads) + [const_b], writes, bias=epsc[:, eps_col:eps_col + 1], scale=scale)
        P.op("dve", lambda e: e.reciprocal(out=out, in_=out), writes, writes)

    def cp(eng, out, in_, reads, writes):
        if eng == "act":
            P.op("act", lambda e: e.activation(out=out, in_=in_, func=AF.Copy), reads, writes)
        else:
            P.op(eng, lambda e: e.tensor_copy(out=out, in_=in_), reads, writes)

    def mm(out, lhsT, rhs, start, stop, reads, writes, signal):
        P.op("pe", lambda e: e.matmul(out, lhsT=lhsT, rhs=rhs, start=start, stop=stop), reads, writes, signal=signal)

    def tr(out, in_, reads, writes, signal):
        P.op("pe", lambda e: e.transpose(out, in_, ident[:]), reads, writes, signal=signal)

    def dbg_dump(name, ap, shape, dt, reads):
        if name not in dbg:
            return
        d = nc.dram_tensor("dbg_" + name, list(shape), dt, kind="ExternalOutput").ap()
        dbg_d[name] = d
        sk = P.dma_sem("dbg")
        P.dma("act", sk, lambda e: e.dma_start(out=d, in_=ap), reads=reads, writes=[Buf()])
        P.wait_tok("act", (sk, 16))

    def cload(eng, dst, src, buf):
        P.dma(eng, csem, lambda e: e.dma_start(out=dst, in_=src), reads=[], writes=[buf])

    cload("sp", ident[:], ident_d, const_b)
    P.op("dve", lambda e: e.memset(ones_bf[:], 1.0), [], [const_b])
    P.op("dve", lambda e: e.memset(ones_f[:], 1.0), [], [const_b])
    P.op("dve", lambda e: e.memset(epsc[:, 0:1], EPS_RMS), [], [const_b])
    P.op("dve", lambda e: e.memset(epsc[:, 1:2], EPS_LN), [], [const_b])
    P.op("dve", lambda e: e.memset(junk, 0.0), [], JUNK_B)
    zsem = [P.dma_sem("zst%d" % i) for i in range(2)]
    zl_sem = P.dma_sem("zld")
    for side in range(2):
        c0 = 0 if side == 0 else PADZ + S
        P.dma("act", zsem[side],
              lambda e, c0=c0: e.dma_start(out=zT_d[:, c0:c0 + PADZ].rearrange("(c p) w -> p c w", p=128),
                                           in_=junk[:, 0:8 * PADZ].rearrange("p (c w) -> p c w", c=8)),
              reads=[ob4], writes=[zpad_buf])

    GROUPS = {"in": [(K0, 2560), (Q0, 1024), (SU0, 2048), (G0, 2048), (G0 + 2048, 2048), (G0 + 4096, 2048)]}
    for k_, (r_, c_) in WSHAPE.items():
        if k_ != "in":
            GROUPS[k_] = [(0, c_)]
    grp_buf = {}
    grp_sem = {}
    conv_q = []
    grp_pending = {}

    def plan_convert(l):
        order = [("in", 0), ("in", 1), ("in", 2), ("in", 3), ("in", 4), ("in", 5), ("ao", 0), ("co", 0), ("so", 0),
                 ("out", 0), ("fg", 0), ("fu", 0), ("fd", 0)]
        for k, gi in order:
            col0, ncols = GROUPS[k][gi]
            r, c = WSHAPE[k]
            gkey = (l, k, gi)
            grp_buf[gkey] = Buf("wb_%s%d_%d" % (k, l, gi))
            sk = grp_sem.get((k, gi))
            if sk is None:
                sk = grp_sem[(k, gi)] = P.dma_sem("cv_%s%d" % (k, gi))
            cw = ncols
            a = 1
            while cw > 2048:
                a += 1
                while ncols % a:
                    a += 1
                cw = ncols // a
            if ncols == c:
                src = w_d[k][l].rearrange("r (a c) -> (r a) c", a=a)
                dst = wb_d[(l, k)].rearrange("r (a c) -> (r a) c", a=a)
                rows = r * a
                per = max(16, ((1 << 20) // cw) // 16 * 16)
            else:
                src = w_d[k][l][:, col0:col0 + ncols].rearrange("r (a c) -> r a c", a=a)
                dst = wb_d[(l, k)][:, col0:col0 + ncols].rearrange("r (a c) -> r a c", a=a)
                rows = r
                per = max(16, ((1 << 20) // ncols) // 16 * 16)
            r0 = 0
            n_d = 0
            while r0 < rows:
                n = min(per, rows - r0)

                def emit_one(sk=sk, src=src, dst=dst, r0=r0, n=n, gkey=gkey):
                    P.dma("pool", sk, lambda e: e.dma_start(out=dst[r0:r0 + n], in_=src[r0:r0 + n]),
                          reads=[], writes=[], serialize=False)
                    grp_buf[gkey].last_write = (sk, P.cnt[sk])
                    grp_buf[gkey].readers = {}
                conv_q.append((gkey, emit_one))
                n_d += 1
                r0 += n
            grp_pending[gkey] = n_d

    def pump_one(paced):
        if not conv_q:
            return False
        gkey, fn = conv_q.pop(0)
        if paced and P.cnt["pe"] > 0:
            P.wait_tok("pool", ("pe", P.cnt["pe"]))
        fn()
        grp_pending[gkey] -= 1
        return True

    def ensure_group(gkey):
        while grp_pending[gkey] > 0:
            pump_one(False)

    pump_state = {"rate": 0.0, "credit": 0.0}

    def pump_tick():
        pump_state["credit"] += pump_state["rate"]
        while pump_state["credit"] >= 1.0:
            pump_state["credit"] -= 1.0
            if not pump_one(True):
                pump_state["credit"] = 0.0
                break

    def group_of(l, k, col0):
        for gi, (c0, nc_) in enumerate(GROUPS[k]):
            if c0 <= col0 < c0 + nc_:
                return (l, k, gi)
        raise KeyError((k, col0))

    slab_ctr = [0]

    def load_sub(l, k, krow0, nk, col0):
        i = slab_ctr[0] % NSLOT
        slab_ctr[0] += 1
        src = wb_d[(l, k)][krow0 * 128:(krow0 + nk) * 128, col0:col0 + 512].rearrange("(kc p) n -> p kc n", p=128)
        gkey = group_of(l, k, col0)
        ensure_group(gkey)
        P.dma("sp", ring_sem[i], lambda e: e.dma_start(out=ring[i][:, 0:nk, :], in_=src),
              reads=[grp_buf[gkey]], writes=[ring_b[i]])
        return ring[i], ring_b[i]

    def linear(l, k, col0, nkc, krow0, rhs_fn, mode, out_fn, extra_reads, banks=None):
        own = banks is None
        if own:
            banks = [newps(hold=True) for _ in range(4)]
        nsub = (nkc + KSUB - 1) // KSUB
        for ss in range(nsub):
            k0 = ss * KSUB
            nk = min(KSUB, nkc - k0)
            slab, slab_b = load_sub(l, k, krow0 + k0, nk, col0)
            for m in range(4):
                ps, psb = banks[m]
                for kk in range(nk):
                    kc = k0 + kk
                    first = own and kc == 0
                    last = own and kc == nkc - 1
                    if mode == "F":
                        mm(ps[:], slab[:, kk, m * 128:(m + 1) * 128], rhs_fn(kc), first, last,
                           [slab_b] + extra_reads, [psb], signal=(kk == nk - 1))
                    else:
                        mm(ps[:], rhs_fn(kc, m), slab[:, kk, :], first, last,
                           [slab_b] + extra_reads, [psb], signal=(kk == nk - 1))
            pump_tick()
        if own and out_fn is not None:
            for m in range(4):
                out_fn(m, banks[m][0], banks[m][1])
                release(banks[m])
        return banks

    gsem = P.dma_sem("grep")

    def norm_to_hT(g_src, src_fn=None):
        if src_fn is None:
            src_fn = lambda s: (xt[:, s, :], [xt_b[s]])
        P.dma("act", gsem, lambda e: e.dma_start(out=g_rep, in_=g_src), reads=[], writes=G_REP_B)
        for s in range(4):
            sap, sbufs = src_fn(s)
            act(junk, sap, AF.Square, sbufs, JUNK_B + [small_b], accum_out=small[:, s:s + 1])
        rsqrt(small[:, 8:12], small[:, 0:4], 1.0 / D, 0, [small_b], [small_b])
        for s in range(4):
            sap, sbufs = src_fn(s)
            stt("dve", xs_t, sap, small[:, 8 + s:9 + s], g_rep, ALU.mult, ALU.mult,
                sbufs + [small_b] + G_REP_B, XS_B)
            for half in range(2):
                ps, psb = newps()
                pv = ps[:].bitcast(BF16)
                for c in range(8):
                    cc = half * 8 + c
                    tr(pv[:, c * 128:(c + 1) * 128], xs_t[:, cc * 128:(cc + 1) * 128], XS_B + [const_b], [psb],
                       signal=(c == 7))
                eng = "act" if half == 0 else "dve"
                cp(eng, hT[:, half * 8:half * 8 + 8, s * 128:(s + 1) * 128],
                   pv.rearrange("p (c t) -> p c t", c=8), [psb], [hT_b])

    def qk_norm_rope(ps, psb, H, gi, s, outb, outb_b):
        W = H * 128
        sq = tmp(0)
        act(sq[:, 0:W], ps[:, 0:W], AF.Square, [psb], [ov2_b[0]])
        P.op("dve", lambda e: e.tensor_reduce(out=small[:, 16:16 + H], in_=sq[:, 0:W].rearrange("p (h d) -> p h d", h=H),
                                              axis=AX.X, op=ALU.add), [ov2_b[0], small_b], [small_b])
        rsqrt(small[:, 24:24 + H], small[:, 16:16 + H], 1.0 / 128, 0, [small_b], [small_b])
        qn = tmp(1)
        qn3 = qn[:, 0:W].rearrange("p (h d) -> p h d", h=H)
        tt("dve", qn3, ps[:, 0:W].rearrange("p (h d) -> p h d", h=H),
           small[:, 24:24 + H].unsqueeze(2).to_broadcast([128, H, 128]), ALU.mult, [psb, small_b], [ov2_b[1]])
        tt("dve", qn3, qn3, qkg[:, gi, :].unsqueeze(1).to_broadcast([128, H, 128]), ALU.mult,
           [ov2_b[1], lconst_b], [ov2_b[1]])
        cosb = cs_t[:, 0, s, :].unsqueeze(1).to_broadcast([128, H, 64])
        sinb = cs_t[:, 1, s, :].unsqueeze(1).to_broadcast([128, H, 64])
        x1 = qn3[:, :, 0:64]
        x2 = qn3[:, :, 64:128]
        o3 = outb[:, 0:W].rearrange("p (h d) -> p h d", h=H)
        t1 = tmp(2)[:, 0:H * 64].rearrange("p (h d) -> p h d", h=H)
        t2 = tmp(2)[:, 256:256 + H * 64].rearrange("p (h d) -> p h d", h=H)
        t3 = tmp(3)[:, 0:H * 64].rearrange("p (h d) -> p h d", h=H)
        t4 = tmp(3)[:, 256:256 + H * 64].rearrange("p (h d) -> p h d", h=H)
        tt("dve", t1, x1, cosb, ALU.mult, [ov2_b[1], cs_b], [ov2_b[2]])
        tt("dve", t2, x2, sinb, ALU.mult, [ov2_b[1], cs_b], [ov2_b[2]])
        tt("dve", o3[:, :, 0:64], t1, t2, ALU.subtract, [ov2_b[2]], [outb_b])
        tt("pool", t3, x2, cosb, ALU.mult, [ov2_b[1], cs_b], [ov2_b[3]])
        tt("pool", t4, x1, sinb, ALU.mult, [ov2_b[1], cs_b], [ov2_b[3]])
        tt("pool", o3[:, :, 64:128], t3, t4, ALU.add, [ov2_b[3]], [outb_b])

    xl_sem = P.dma_sem("xld")
    xst_sem = P.dma_sem("xst")
    cs_sem = P.dma_sem("cs")
    lc_sem = P.dma_sem("lconst")

    pn_sem = [P.dma_sem("pn%d" % i) for i in range(2)]
    xsA = mgT[:].rearrange("p a b -> p (a b)").bitcast(F32).rearrange("p (s d) -> p s d", s=2)
    xsB = stg[:].rearrange("p a b -> p (a b)").rearrange("p (s d) -> p s d", s=2)

    def prenorm(src_ap, src_bufs, t1, g_src):
        P.dma("sp", pn_sem[0],
              lambda e: e.dma_start(out=xsA, in_=src_ap[t1 * T:t1 * T + 256, :].rearrange("(s p) d -> p s d", p=128)),
              reads=src_bufs, writes=mg_b)
        P.dma("sp", pn_sem[1],
              lambda e: e.dma_start(out=xsB, in_=src_ap[t1 * T + 256:t1 * T + 512, :].rearrange("(s p) d -> p s d", p=128)),
              reads=src_bufs, writes=stg_b)
        norm_to_hT(g_src, lambda s: ((xsA[:, s, :], list(mg_b)) if s < 2 else (xsB[:, s - 2, :], list(stg_b))))

    def load_x_tile(src_ap, src_bufs, t, q="act"):
        P.dma(q, xl_sem,
              lambda e: e.dma_start(out=xt[:], in_=src_ap[t * T:(t + 1) * T, :].rearrange("(s p) d -> p s d", p=128)),
              reads=src_bufs, writes=xt_b)

    def load_cs(t):
        P.dma("act", cs_sem, lambda e: e.dma_start(out=cs_t[:, 0, :, :], in_=cos_d[:, t * 4:(t + 1) * 4, :]),
              reads=[], writes=[cs_b])
        P.dma("act", cs_sem, lambda e: e.dma_start(out=cs_t[:, 1, :, :], in_=sin_d[:, t * 4:(t + 1) * 4, :]),
              reads=[], writes=[cs_b])

    plan_convert(0)
    pump_state["rate"] = 0.85

    for l in range(L):
        x_src = x_d if l == 0 else xs_d[l - 1]
        x_src_b = (lambda t: []) if l == 0 else (lambda t, l=l: [xs_buf[l - 1][t]])
        last_layer = (l == L - 1)

        def lload(dst, src):
            P.dma("act", lc_sem, lambda e: e.dma_start(out=dst, in_=src), reads=[], writes=[lconst_b])
        lload(bgate[:], bgate_d[l])
        lload(qkg[:, 0, :], qg_d[l])
        lload(qkg[:, 1, :], kg_d[l])
        lload(wdw[:], wdw_d[l])
        lload(cvec[:], cvec_d[l])
        P.dma("act", lc_sem, lambda e, l=l: e.dma_start(out=stg[:, 0:2, :].rearrange("p a t -> p (a t)").rearrange("p (g q) -> p g q", g=8), in_=wsT_d[l]),
              reads=[], writes=stg_b[0:4] + [lconst_b])
        P.dma("act", lc_sem, lambda e, l=l: e.dma_start(out=stg[:, 2:4, :].rearrange("p a t -> p (a t)").rearrange("p (g q) -> p g q", g=8), in_=bs_d[l]),
              reads=[], writes=stg_b[0:4] + [lconst_b])
        wsf = stg[:, 0:2, :].rearrange("p a t -> p (a t)").rearrange("p (g q) -> p g q", g=8)
        bsr = stg[:, 2:4, :].rearrange("p a t -> p (a t)").rearrange("p (g q) -> p g q", g=8)
        cp("dve", wsT_bf[:], wsf, [lconst_b] + stg_b[0:4], [lconst_b])
        for gq in range(2):
            ps, psb = newps()
            for g4 in range(4):
                g = gq * 4 + g4
                mm(ps[:, g4 * 128:(g4 + 1) * 128], ones_f[:], wsf[:, g, :], True, True, [const_b, lconst_b] + stg_b[0:4],
                   [psb], signal=(g4 == 3))
            for g4 in range(4):
                g = gq * 4 + g4
                stt("dve", Bmat[:, g, :], ps[:, g4 * 128:(g4 + 1) * 128], cvec[:, 4, g:g + 1], bsr[:, g, :],
                    ALU.mult, ALU.add, [psb, lconst_b] + stg_b[0:4], [lconst_b])

        if l == 0:
            dbg_dump("Bmat", Bmat[:], [128, 8, 128], F32, [lconst_b])
            dbg_dump("wsTbf", wsT_bf[:], [128, 8, 128], BF16, [lconst_b])
            dbg_dump("cvec", cvec[:], [128, 5, 8], F32, [lconst_b])
        for t in range(NT):
            load_x_tile(x_src, x_src_b(t), t)
            load_cs(t)
            norm_to_hT(gmix_d[l])
            if l == 0 and t == 0:
                dbg_dump("hT", hT[:], [128, 16, T], BF16, [hT_b])

            def kv_out(s, ps, psb, t=t):
                ob, obb = qnb[s % 2], qnb_b[s % 2]
                qk_norm_rope(ps, psb, 2, 1, s, ob, obb)
                cp("act", Vt[:, t * 4 + s, :], ps[:, 256:512], [psb], [Vt_b])
                p2, p2b = newps()
                pv = p2[:].bitcast(BF16)
                for h in range(2):
                    tr(pv[:, h * 128:(h + 1) * 128], ob[:, h * 128:(h + 1) * 128], [obb, const_b], [p2b], signal=(h == 1))
                cp("act", kT[:, :, t * T + s * 128:t * T + (s + 1) * 128],
                   pv[:, 0:256].rearrange("p (h t) -> p h t", h=2), [p2b], [kT_b])
            linear(l, "in", K0, 16, 0, lambda kc, s: hT[:, kc, s * 128:(s + 1) * 128], "T", kv_out, [hT_b])

            for j in range(2):
                def g_out(m, ps, psb):
                    act(stg[:, m, :], ps[:], AF.Sigmoid, [psb], [stg_b[m]])
                linear(l, "in", CG0 + 512 * j, 16, 0, lambda kc: hT[:, kc, :], "F", g_out, [hT_b])

                def a_out(m, ps, psb, j=j):
                    tt("dve", brT[:, m, :], ps[:], stg[:, m, :], ALU.mult, [psb, stg_b[m]], [br_b[m]])
                linear(l, "in", CA0 + 512 * j, 16, 0, lambda kc: hT[:, kc, :], "F", a_out, [hT_b])
                c0 = PADZ + t * T
                P.dma("act", zsem[j],
                      lambda e, j=j, c0=c0: e.dma_start(
                          out=zT_d[j * 512:(j + 1) * 512, c0:c0 + T].rearrange("(c p) w -> p c w", p=128),
                          in_=brT[:, 0:4, :]),
                      reads=br_b[0:4], writes=[z_buf[t][j]])
        if l == 0:
            dbg_dump("kT", kT[:], [128, 2, S], BF16, [kT_b])
            dbg_dump("Vt", Vt[:], [128, 16, 256], BF16, [Vt_b])

        def finalize_tile(tp, l=l, last_layer=last_layer):
            if not last_layer:
                P.dma("sp", xst_sem,
                      lambda e: e.dma_start(out=xs_d[l][tp * T:(tp + 1) * T, :].rearrange("(s p) d -> p s d", p=128),
                                            in_=xt[:]),
                      reads=xt_b, writes=xs_buf[l][tp] if isinstance(xs_buf[l][tp], list) else [xs_buf[l][tp]])
            else:
                P.dma("act", gsem, lambda e: e.dma_start(out=g_rep, in_=gfin_d), reads=[], writes=G_REP_B)
                for s in range(4):
                    act(junk, xt[:, s, :], AF.Square, [xt_b[s]], JUNK_B + [small_b], accum_out=small[:, s:s + 1])
                rsqrt(small[:, 8:12], small[:, 0:4], 1.0 / D, 0, [small_b], [small_b])
                for s in range(4):
                    stt("dve", xt[:, s, :], xt[:, s, :], small[:, 8 + s:9 + s], g_rep, ALU.mult, ALU.mult,
                        [xt_b[s], small_b] + G_REP_B, [xt_b[s]])
                P.dma("sp", xst_sem,
                      lambda e: e.dma_start(out=out_d[tp * T:(tp + 1) * T, :].rearrange("(s p) d -> p s d", p=128),
                                            in_=xt[:]),
                      reads=xt_b, writes=[Buf()])

        for t in range(ntilesB):
            if l == 0 and t == 0:
                pump_state["rate"] = 0.75
            if l == 0 and t == 1 and L > 1:
                while pump_one(True):
                    pass
                plan_convert(1)
                pump_state["rate"] = 0.2
            load_cs(t)
            if t == 0:
                load_x_tile(x_src, x_src_b(t), t)
                norm_to_hT(gmix_d[l])

            for j in range(2):
                def q_out(s, ps, psb, j=j):
                    ob, obb = qnb[s % 2], qnb_b[s % 2]
                    qk_norm_rope(ps, psb, 4, 0, s, ob, obb)
                    p2, p2b = newps()
                    pv = p2[:].bitcast(BF16)
                    for h in range(4):
                        tr(pv[:, h * 128:(h + 1) * 128], ob[:, h * 128:(h + 1) * 128], [obb, const_b], [p2b],
                           signal=(h == 3))
                    cp("act", mgT[:, 4 * j:4 * j + 4, s * 128:(s + 1) * 128],
                       pv[:, 0:512].rearrange("p (h t) -> p h t", h=4), [p2b], mg_b[4 * j:4 * j + 4])
                linear(l, "in", Q0 + 512 * j, 16, 0, lambda kc, s: hT[:, kc, s * 128:(s + 1) * 128], "T", q_out, [hT_b])
            if t > 0:
                finalize_tile(t - 1)
                load_x_tile(x_src, x_src_b(t), t, q="sp")
            if l == 0 and t == 0:
                dbg_dump("qT", mgT[:, 0:8, :], [128, 8, T], BF16, mg_b[0:8])

            LOOK = 3
            NIT = 8 * 16

            def emit_scores(i):
                h, kc = divmod(i, 16)
                kvh = h // 4
                ps, psb = newps()
                mm(ps[:], kT[:, kvh, kc * 128:(kc + 1) * 128], mgT[:, h, :], True, True, [kT_b, mg_b[h]], [psb], True)
                ptt, pttb = pt[i % 4], pt_b[i % 4]
                act(ptt[:], ps[:], AF.Exp, [psb], [pttb], scale=SM_SCALE)

            for i in range(LOOK):
                emit_scores(i)
            for i in range(NIT):
                h, kc = divmod(i, 16)
                kvh = h // 4
                if kc == 0:
                    pso, psob = newps(hold=True)
                    pss, pssb = newps(hold=True)
                ptt, pttb = pt[i % 4], pt_b[i % 4]
                mm(pso[:], Vt[:, kc, kvh * 128:(kvh + 1) * 128], ptt[:], kc == 0, kc == 15, [Vt_b, pttb], [psob], False)
                mm(pss[:], ones_bf[:], ptt[:], kc == 0, kc == 15, [const_b, pttb], [pssb], True)
                if i + LOOK < NIT:
                    emit_scores(i + LOOK)
                if kc == 15:
                    P.op("dve", lambda e, pss=pss: e.reciprocal(out=tmp(5), in_=pss[:]), [pssb], [ov2_b[5]])
                    tt("dve", brT[:, h, :], pso[:], tmp(5), ALU.mult, [psob, ov2_b[5]], [br_b[h]])
                    release((pso, psob), (pss, pssb))
            if l == 0 and t == 0:
                dbg_dump("attnT", brT[:, 0:8, :], [128, 8, T], BF16, br_b[0:8])

            tl = [z_buf[tt_][jj] for tt_ in range(max(0, t - 1), min(NT, t + 2)) for jj in range(2)]
            P.dma("act", zl_sem,
                  lambda e, t=t: e.dma_start(out=zw[:, :, 0:T + 2 * PADZ],
                                             in_=zT_d[:, t * T:t * T + T + 2 * PADZ].rearrange("(c p) w -> p c w", p=128)),
                  reads=tl + [zpad_buf], writes=ZW_B)
            psm, psmb = newps(hold=True)
            psq, psqb = newps(hold=True)
            for c in range(8):
                P.op("pool", lambda e, c=c: e.tensor_tensor(
                    out=diag[:, 0:16, :], in0=ident[:].unsqueeze(1).to_broadcast([128, 16, 128]),
                    in1=wdw[:, c, 0:16].unsqueeze(2).to_broadcast([128, 16, 128]), op=ALU.mult),
                    [const_b, lconst_b], DLO_B)
                P.op("pool", lambda e, c=c: e.tensor_tensor(
                    out=diag[:, 16:31, :], in0=ident[:].unsqueeze(1).to_broadcast([128, 15, 128]),
                    in1=wdw[:, c, 16:31].unsqueeze(2).to_broadcast([128, 15, 128]), op=ALU.mult),
                    [const_b, lconst_b], DHI_B)
                ps, psb = newps()
                for kk in range(31):
                    mm(ps[:], diag[:, kk, :], zw[:, c, kk:kk + T], kk == 0, kk == 30,
                       ZW_B + (DLO_B if kk < 16 else DHI_B), [psb], kk in (15, 30))
                act(stg[:, c, :], ps[:], AF.Identity, [psb, lconst_b], [stg_b[c]], bias=cvec[:, 0, c:c + 1])
                act(gt[c % 2][:], stg[:, c, :], AF.Square, [stg_b[c]], [gt_b[c % 2]])
                mm(psm[:], ones_f[:], stg[:, c, :], c == 0, c == 7, [const_b, stg_b[c]], [psmb], True)
                mm(psq[:], ones_f[:], gt[c % 2][:], c == 0, c == 7, [const_b, gt_b[c % 2]], [psqb], True)
            mean = tmp(0)
            rstd = tmp(1)
            ts("dve", mean, psm[:], 1.0 / 1024, None, ALU.mult, None, [psmb], [ov2_b[0]])
            tt("dve", tmp(2), mean, mean, ALU.mult, [ov2_b[0]], [ov2_b[2]])
            stt("dve", rstd, psq[:], 1.0 / 1024, tmp(2), ALU.mult, ALU.subtract, [psqb, ov2_b[2]], [ov2_b[1]])
            rsqrt(rstd, rstd, 1.0, 1, [ov2_b[1]], [ov2_b[1]])
            release((psm, psmb), (psq, psqb))
            for c in range(8):
                u = pr[c % 2]
                ub = pr_b[c % 2]
                tt("dve", u[:], stg[:, c, :], mean, ALU.subtract, [stg_b[c], ov2_b[0]], [ub])
                tt("dve", u[:], u[:], rstd, ALU.mult, [ub, ov2_b[1]], [ub])
                act(brT[:, 8 + c, :], u[:], AF.Silu, [ub, lconst_b], [br_b[8 + c]],
                    bias=cvec[:, 2, c:c + 1], scale=cvec[:, 1, c:c + 1])
            if l == 0 and t == 0:
                dbg_dump("convT", brT[:, 8:16, :], [128, 8, T], BF16, br_b[8:16])

            def gelu(dst, src_ps, psb, dstb, k1, k2):
                a1, a2 = tmp(k1), tmp(k2)
                act(a1, src_ps, AF.Square, [psb], [ov2_b[k1]])
                ts("dve", a1, a1, GELU_C, 1.0, ALU.mult, ALU.add, [ov2_b[k1]], [ov2_b[k1]])
                tt("dve", a1, a1, src_ps, ALU.mult, [ov2_b[k1], psb], [ov2_b[k1]])
                act(a2, a1, AF.Sigmoid, [ov2_b[k1]], [ov2_b[k2]], scale=GELU_S)
                tt("dve", dst, a2, src_ps, ALU.mult, [ov2_b[k2], psb], [dstb])

            for j in range(2):
                def v_out(s, ps, psb, j=j):
                    gv = tmp(4)
                    gelu(gv, ps[:], psb, ov2_b[4], 0, 1)
                    gv3 = gv.rearrange("p (g c) -> p g c", g=4)
                    P.op("dve", lambda e: e.tensor_reduce(out=small[:, 32:36], in_=gv3, axis=AX.X, op=ALU.add),
                         [ov2_b[4], small_b], [small_b])
                    act(tmp(5), gv, AF.Square, [ov2_b[4]], [ov2_b[5]])
                    P.op("dve", lambda e: e.tensor_reduce(out=small[:, 36:40],
                                                          in_=tmp(5).rearrange("p (g c) -> p g c", g=4), axis=AX.X,
                                                          op=ALU.add), [ov2_b[5], small_b], [small_b])
                    ts("dve", small[:, 32:36], small[:, 32:36], 1.0 / 128, None, ALU.mult, None, [small_b], [small_b])
                    tt("dve", small[:, 40:44], small[:, 32:36], small[:, 32:36], ALU.mult, [small_b], [small_b])
                    stt("dve", small[:, 36:40], small[:, 36:40], 1.0 / 128, small[:, 40:44], ALU.mult, ALU.subtract,
                        [small_b], [small_b])
                    rsqrt(small[:, 36:40], small[:, 36:40], 1.0, 1, [small_b], [small_b])
                    tt("dve", gv3, gv3, small[:, 32:36].unsqueeze(2).to_broadcast([128, 4, 128]), ALU.subtract,
                       [ov2_b[4], small_b], [ov2_b[4]])
                    vb, vbb = qnb[s % 2], qnb_b[s % 2]
                    tt("dve", vb[:].rearrange("p (g c) -> p g c", g=4), gv3,
                       small[:, 36:40].unsqueeze(2).to_broadcast([128, 4, 128]), ALU.mult, [ov2_b[4], small_b], [vbb])
                    sp, spb_ = newps()
                    for g4 in range(4):
                        g = 4 * j + g4
                        mm(sp[:, g4 * 128:(g4 + 1) * 128], vb[:, g4 * 128:(g4 + 1) * 128], wsT_bf[:, g, :], True, True,
                           [vbb, lconst_b], [spb_], g4 == 3)
                    for g4 in range(4):
                        g = 4 * j + g4
                        stt("dve", stg[:, g4, s * 128:(s + 1) * 128], sp[:, g4 * 128:(g4 + 1) * 128], cvec[:, 3, g:g + 1],
                            Bmat[:, g, :], ALU.mult, ALU.add, [spb_, lconst_b], [stg_b[g4]])
                linear(l, "in", SV0 + 512 * j, 16, 0, lambda kc, s: hT[:, kc, s * 128:(s + 1) * 128], "T", v_out, [hT_b])

                def u_out(m, ps, psb, j=j):
                    ug = tmp(4)
                    gelu(ug, ps[:], psb, ov2_b[4], 2, 3)
                    tt("dve", brT[:, 16 + 4 * j + m, :], ug, stg[:, m, :], ALU.mult, [ov2_b[4], stg_b[m]],
                       [br_b[16 + 4 * j + m]])
                linear(l, "in", SU0 + 512 * j, 16, 0, lambda kc: hT[:, kc, :], "F", u_out, [hT_b])
            if l == 0 and t == 0:
                dbg_dump("sgT", brT[:, 16:24, :], [128, 8, T], BF16, br_b[16:24])


            for j in range(4):
                for i, bk in enumerate(("ao", "co", "so")):
                    def gate_out(m, ps, psb, i=i, j=j):
                        act(gt[m % 2][:], ps[:], AF.Sigmoid, [psb, lconst_b], [gt_b[m % 2]],
                            bias=bgate[:, i * 16 + j * 4 + m:i * 16 + j * 4 + m + 1])
                    gbanks = linear(l, "in", G0 + i * 2048 + j * 512, 16, 0, lambda kc: hT[:, kc, :], "F",
                                    None, [hT_b])
                    ybanks = linear(l, bk, j * 512, 8, 0, lambda kc, i=i: brT[:, 8 * i + kc, :], "F",
                                    None, br_b[8 * i:8 * i + 8])
                    for m in range(4):
                        gate_out(m, gbanks[m][0], gbanks[m][1])
                        yps, ypsb = ybanks[m]
                        if i == 0:
                            tt("dve", stg[:, 4 + m, :], gt[m % 2][:], yps[:], ALU.mult, [gt_b[m % 2], ypsb], [stg_b[4 + m]])
                        else:
                            prod, prodb = pr[m % 2], pr_b[m % 2]
                            tt("dve", prod[:], gt[m % 2][:], yps[:], ALU.mult, [gt_b[m % 2], ypsb], [prodb])
                            if i == 1:
                                tt("dve", stg[:, 4 + m, :], stg[:, 4 + m, :], prod[:], ALU.add, [stg_b[4 + m], prodb],
                                   [stg_b[4 + m]])
                            else:
                                tt("dve", mgT[:, 4 * j + m, :], stg[:, 4 + m, :], prod[:], ALU.add,
                                   [stg_b[4 + m], prodb], [mg_b[4 * j + m]])
                    release(*gbanks)
                    release(*ybanks)
            if l == 0 and t == 0:
                dbg_dump("mgT", mgT[:], [128, 16, T], BF16, mg_b)

            for j in range(4):
                def o_out(s, ps, psb, j=j):
                    tt("dve", xt[:, s, j * 512:(j + 1) * 512], xt[:, s, j * 512:(j + 1) * 512], ps[:], ALU.add,
                       [xt_b[s], psb], [xt_b[s]])
                linear(l, "out", j * 512, 16, 0, lambda kc, s: mgT[:, kc, s * 128:(s + 1) * 128], "T", o_out, mg_b)
            if l == 0 and t == 0:
                dbg_dump("xmix", xt[:], [128, 4, D], F32, xt_b)

            norm_to_hT(gffn_d[l])
            for (j0, j1) in ((0, 6), (6, 11)):
                for j in range(j0, j1):
                    gb = linear(l, "fg", j * 512, 16, 0, lambda kc: hT[:, kc, :], "F", None, [hT_b])
                    ub_ = linear(l, "fu", j * 512, 16, 0, lambda kc: hT[:, kc, :], "F", None, [hT_b])
                    for m in range(4):
                        act(gt[m % 2][:], gb[m][0][:], AF.Silu, [gb[m][1]], [gt_b[m % 2]])
                        ci = (j - j0) * 4 + m
                        tt("dve", brT[:, ci, :], gt[m % 2][:], ub_[m][0][:], ALU.mult, [gt_b[m % 2], ub_[m][1]], [br_b[ci]])
                    release(*gb)
                    release(*ub_)
                if j0 == 6 and t + 1 < ntilesB:
                    prenorm(x_src, x_src_b(t + 1), t + 1, gmix_d[l])
                nkc = (j1 - j0) * 4
                for jo in range(4):
                    def d_out(s, ps, psb, jo=jo):
                        tt("dve", xt[:, s, jo * 512:(jo + 1) * 512], xt[:, s, jo * 512:(jo + 1) * 512], ps[:], ALU.add,
                           [xt_b[s], psb], [xt_b[s]])
                    linear(l, "fd", jo * 512, nkc, j0 * 4, lambda kc, s: brT[:, kc, s * 128:(s + 1) * 128], "T", d_out,
                           br_b[0:nkc])

            if t == ntilesB - 1:
                finalize_tile(t)

    for key in (xst_sem, zsem[0], zsem[1]):
        if P.cnt[key] > 0:
            P.wait_tok("act", (key, P.cnt[key]))
    P.emit()
    return nc, dbg_d


def _rope_tables():
    rows = S // 64
    row = np.repeat(np.arange(rows, dtype=np.float32), 64)
    col = np.tile(np.arange(64, dtype=np.float32), rows)
    inv = (np.float32(10000.0) ** (-np.arange(32, dtype=np.float32) / np.float32(32))).astype(np.float32)
    ang = np.concatenate([row[:, None] * inv, col[:, None] * inv], axis=-1).astype(np.float32)
    cos = np.cos(ang).astype(np.float32)
    sin = np.sin(ang).astype(np.float32)
    return (np.ascontiguousarray(cos.reshape(16, 128, 64).transpose(1, 0, 2)),
            np.ascontiguousarray(sin.reshape(16, 128, 64).transpose(1, 0, 2)))


def _prep_shared(inp, L):
    f = lambda a: np.ascontiguousarray(np.asarray(a, dtype=np.float32))
    sh = {}
    sh["w_in"] = f(inp["w_in"][:L])
    sh["w_attn_o"] = f(inp["w_attn_o"][:L])
    sh["w_conv_o"] = f(inp["w_conv_o"][:L])
    sh["w_sg_o"] = f(inp["w_sg_o"][:L])
    sh["w_out"] = f(inp["w_out"][:L])
    sh["w_ff_gate"] = f(inp["w_ff_gate"][:L])
    sh["w_ff_up"] = f(inp["w_ff_up"][:L])
    sh["w_ff_down"] = f(inp["w_ff_down"][:L])
    rep = lambda v: np.ascontiguousarray(np.broadcast_to(np.asarray(v, np.float32)[..., None, :],
                                                         v.shape[:-1] + (128, v.shape[-1])))
    sh["g_mix_rep"] = rep(np.asarray(inp["g_mix"])[:L])
    sh["g_ffn_rep"] = rep(np.asarray(inp["g_ffn"])[:L])
    sh["g_final_rep"] = rep(np.asarray(inp["g_final"]))
    colT = lambda v, n: np.ascontiguousarray(np.asarray(v, np.float32).reshape(v.shape[0], n, 128).transpose(0, 2, 1))
    sh["b_gate_T"] = colT(np.asarray(inp["b_gate"])[:L], 48)
    sh["qg_rep"] = rep(np.asarray(inp["q_norm_g"])[:L])
    sh["kg_rep"] = rep(np.asarray(inp["k_norm_g"])[:L])
    wdw = np.asarray(inp["w_dw"], np.float32)[:L, :, 0, :]
    sh["wdw_T"] = np.ascontiguousarray(wdw.reshape(L, 31, 8, 128).transpose(0, 3, 2, 1))
    vecs = [inp["b_dw"], inp["conv_ln_g"], inp["conv_ln_b"], inp["sg_ln_g"], inp["sg_ln_b"]]
    sh["cvec_T"] = np.ascontiguousarray(np.stack([colT(np.asarray(v)[:L], 8) for v in vecs], axis=2))
    ws = np.asarray(inp["w_s"], np.float32)[:L]
    sh["wsT"] = np.ascontiguousarray(ws.transpose(0, 3, 1, 2))
    bs = np.asarray(inp["b_s"], np.float32)[:L]
    sh["bs_rep"] = np.ascontiguousarray(np.broadcast_to(bs[:, None, :, :], (L, 128, 8, 128)))
    c, s = _rope_tables()
    sh["cos_t"] = c
    sh["sin_t"] = s
    sh["ident_bf"] = np.eye(128, dtype=np.float32).astype(ml_dtypes.bfloat16)
    return sh


_CACHE = {}


def kernel(**inputs):
    L = DEPTH
    if "nc" not in _CACHE:
        _CACHE["nc"] = build(L)[0]
    nc = _CACHE["nc"]
    sh = _prep_shared(inputs, L)
    x = np.asarray(inputs["x"], dtype=np.float32)
    in_maps = []
    for b in range(NB):
        m = dict(sh)
        m["x"] = np.ascontiguousarray(x[b])
        in_maps.append(m)
    res = run_bass_kernel_spmd(nc, in_maps, core_ids=list(range(NB)))
    out = np.stack([np.asarray(res.results[b]["out"], dtype=np.float32) for b in range(NB)], axis=0)
    return out
```

```python
import numpy as np
import ml_dtypes
import concourse.bass as bass
import concourse.mybir as mybir
from concourse.bass_utils import run_bass_kernel_spmd

F32 = mybir.dt.float32
BF16 = mybir.dt.bfloat16
AF = mybir.ActivationFunctionType
ALU = mybir.AluOpType
AX = mybir.AxisListType

D = 2048
S = 2048
NB = 8
DEPTH = 2
T = 512
NT = S // T
IN_COLS = 11776
Q0, K0, V0, CA0, CG0, SU0, SV0, G0 = 0, 1024, 1280, 1536, 2560, 3584, 4608, 5632
DFF = 5632
KSUB = 8
NSLOT = 4
PADZ = 15
ZW = S + 2 * PADZ
EPS_RMS = 1e-6
EPS_LN = 1e-5
SM_SCALE = 128 ** -0.5
GELU_C = 0.044715
GELU_S = 1.5957691216057308


class Buf:
    __slots__ = ("name", "last_write", "readers")

    def __init__(self, name=""):
        self.name = name
        self.last_write = None
        self.readers = {}


class Prog:
    ENGS = ("pe", "act", "dve", "pool", "sp")

    def __init__(self, nc, same_eng_sync=True):
        self.nc = nc
        self.same_eng_sync = same_eng_sync
        self.sems = {}
        self.cnt = {}
        for e in ("pe", "act", "dve", "pool"):
            self.sems[e] = nc.alloc_semaphore("s_" + e)
            self.cnt[e] = 0
        self.waited = {e: {} for e in self.ENGS}
        self.q = {e: [] for e in self.ENGS}
        self.n_dma_sem = 0

    def dma_sem(self, name=""):
        self.n_dma_sem += 1
        key = "d%d_%s" % (self.n_dma_sem, name)
        self.sems[key] = self.nc.alloc_semaphore(key)
        self.cnt[key] = 0
        return key

    def _deps(self, eng, reads, writes):
        need = {}

        def add(t):
            if t is None:
                return
            k, v = t
            if need.get(k, 0) < v:
                need[k] = v
        for b in reads:
            add(b.last_write)
        for b in writes:
            add(b.last_write)
            for k, v in b.readers.items():
                add((k, v))
        for k, v in need.items():
            if k == eng and (eng == "pe" or not self.same_eng_sync):
                continue
            if k == eng and v > self.cnt[eng]:
                continue
            self.wait_tok(eng, (k, v))

    def _mark(self, tok, reads, writes):
        k, v = tok
        for b in reads:
            if b.readers.get(k, 0) < v:
                b.readers[k] = v
        for b in writes:
            b.last_write = tok
            b.readers = {}

    def op(self, eng, fn, reads=(), writes=(), signal=True):
        self._deps(eng, reads, writes)
        if signal:
            self.cnt[eng] += 1
            tok = (eng, self.cnt[eng])
        else:
            tok = (eng, self.cnt[eng] + 1)
        self.q[eng].append(("op", fn, signal, eng, 1))
        self._mark(tok, reads, writes)
        return tok

    def dma(self, eng, semkey, fn, reads=(), writes=(), serialize=True):
        self._deps(eng, reads, writes)
        if serialize and self.cnt[semkey] > 0:
            self.wait_tok(eng, (semkey, self.cnt[semkey]))
        self.cnt[semkey] += 16
        tok = (semkey, self.cnt[semkey])
        self.q[eng].append(("op", fn, True, semkey, 16))
        self._mark(tok, reads, writes)
        return tok

    def wait_tok(self, eng, tok):
        k, v = tok
        if self.waited[eng].get(k, 0) < v:
            self.waited[eng][k] = v
            self.q[eng].append(("wait", k, v))

    def emit(self):
        nc = self.nc
        for e in self.ENGS:
            for item in self.q[e]:
                if item[0] == "wait":
                    assert item[2] <= self.cnt[item[1]], ("unreachable wait", e, item)
        with nc.Block() as block:
            def runner(ename):
                def run(engine):
                    for item in self.q[ename]:
                        if item[0] == "wait":
                            engine.wait_ge(self.sems[item[1]], item[2])
                        else:
                            _, fn, signal, semk, inc = item
                            ins = fn(engine)
                            if signal:
                                ins.then_inc(self.sems[semk], inc)
                return run
            block.sync(runner("sp"))
            block.tensor(runner("pe"))
            block.scalar(runner("act"))
            block.vector(runner("dve"))
            block.gpsimd(runner("pool"))


def build(nlayers=DEPTH, ntilesB=NT, dbg=()):
    nc = bass.Bass("TRN2", target_bir_lowering=False)
    P = Prog(nc)
    L = nlayers

    def din(name, shape, dt=F32):
        return nc.dram_tensor(name, list(shape), dt, kind="ExternalInput").ap()

    x_d = din("x", [S, D])
    w_d = {
        "in": din("w_in", [L, D, IN_COLS]),
        "ao": din("w_attn_o", [L, 1024, D]),
        "co": din("w_conv_o", [L, 1024, D]),
        "so": din("w_sg_o", [L, 1024, D]),
        "out": din("w_out", [L, D, D]),
        "fg": din("w_ff_gate", [L, D, DFF]),
        "fu": din("w_ff_up", [L, D, DFF]),
        "fd": din("w_ff_down", [L, DFF, D]),
    }
    WSHAPE = {"in": (D, IN_COLS), "ao": (1024, D), "co": (1024, D), "so": (1024, D), "out": (D, D),
              "fg": (D, DFF), "fu": (D, DFF), "fd": (DFF, D)}
    gmix_d = din("g_mix_rep", [L, 128, D])
    gffn_d = din("g_ffn_rep", [L, 128, D])
    gfin_d = din("g_final_rep", [128, D])
    bgate_d = din("b_gate_T", [L, 128, 48])
    qg_d = din("qg_rep", [L, 128, 128])
    kg_d = din("kg_rep", [L, 128, 128])
    wdw_d = din("wdw_T", [L, 128, 8, 31])
    cvec_d = din("cvec_T", [L, 128, 5, 8])
    wsT_d = din("wsT", [L, 128, 8, 128])
    bs_d = din("bs_rep", [L, 128, 8, 128])
    cos_d = din("cos_t", [128, 16, 64])
    sin_d = din("sin_t", [128, 16, 64])
    ident_d = din("ident_bf", [128, 128], BF16)
    out_d = nc.dram_tensor("out", [S, D], F32, kind="ExternalOutput").ap()
    dbg_d = {}

    wb_d = {}
    wb_buf = {}
    for l in range(L):
        for k, (r, c) in WSHAPE.items():
            wb_d[(l, k)] = nc.dram_tensor("wb_%s%d" % (k, l), [r, c], BF16).ap()
            wb_buf[(l, k)] = Buf("wb_%s%d" % (k, l))
    xs_d = [nc.dram_tensor("xscr%d" % i, [S, D], F32, kind=("ExternalOutput" if "xscr" in dbg else "Internal")).ap()
            for i in range(max(1, L - 1))]
    xs_buf = [[Buf("xscr%d_%d" % (i, t)) for t in range(NT)] for i in range(max(1, L - 1))]
    zT_d = nc.dram_tensor("zT", [1024, ZW], BF16).ap()
    z_buf = [[Buf("z%d_%d" % (t, j)) for j in range(2)] for t in range(NT)]
    zpad_buf = Buf("zpad")

    def sb(name, shape, dt):
        return nc.alloc_sbuf_tensor(name, list(shape), dt)

    xt = sb("xt", [128, 4, D], F32)
    xt_b = [Buf("xt%d" % s) for s in range(4)]
    hT = sb("hT", [128, 16, T], BF16)
    hT_b = Buf("hT")
    kT = sb("kT", [128, 2, S], BF16)
    kT_b = Buf("kT")
    Vt = sb("Vt", [128, 16, 256], BF16)
    Vt_b = Buf("Vt")
    brT = sb("brT", [128, 24, T], BF16)
    br_b = [Buf("br%d" % i) for i in range(24)]
    mgT = sb("mgT", [128, 16, T], BF16)
    mg_b = [Buf("mg%d" % i) for i in range(16)]
    ring = [sb("ring%d" % i, [128, KSUB, 512], BF16) for i in range(NSLOT)]
    ring_b = [Buf("ring%d" % i) for i in range(NSLOT)]
    ring_sem = [P.dma_sem("ring%d" % i) for i in range(NSLOT)]
    stg = sb("stg", [128, 8, T], F32)
    stg_b = [Buf("stg%d" % i) for i in range(8)]
    OV1 = 16896
    ov1 = sb("ov1", [128, OV1 // 2], BF16)
    ob1, ob2, ob3, ob4, ob5 = [Buf("ov1_%d" % i) for i in range(5)]
    G_REP_B, XS_B, JUNK_B, ZW_B, DLO_B, DHI_B = [ob1], [ob2, ob3], [ob4, ob5], [ob1, ob2], [ob3, ob4], [ob5]
    g_rep = ov1[:, 0:4096].bitcast(F32)
    xs_t = ov1[:, 4096:6144]
    junk = ov1[:, 6144:8192]
    zw = ov1[:, 0:8 * 544].rearrange("p (c w) -> p c w", c=8)
    diag = ov1[:, 4352:4352 + 31 * 128].rearrange("p (k j) -> p k j", k=31)
    ov2 = sb("ov2", [128, 6 * 512], F32)
    ov2_b = [Buf("ov2_%d" % i) for i in range(6)]

    def tmp(i):
        return ov2[:, i * 512:(i + 1) * 512]
    pt = [sb("pt%d" % i, [128, 512], BF16) for i in range(4)]
    pt_b = [Buf("pt%d" % i) for i in range(4)]
    gt = [sb("gt%d" % i, [128, 512], F32) for i in range(2)]
    gt_b = [Buf("gt%d" % i) for i in range(2)]
    pr = [sb("pr%d" % i, [128, 512], F32) for i in range(2)]
    pr_b = [Buf("pr%d" % i) for i in range(2)]
    qnb = [sb("qnb%d" % i, [128, 512], BF16) for i in range(2)]
    qnb_b = [Buf("qnb%d" % i) for i in range(2)]
    small = sb("small", [128, 64], F32)
    small_b = Buf("small")
    ident = sb("ident", [128, 128], BF16)
    ones_bf = sb("ones_bf", [128, 128], BF16)
    ones_f = sb("ones_f", [128, 128], F32)
    epsc = sb("epsc", [128, 2], F32)
    cs_t = sb("cs_t", [128, 2, 4, 64], F32)
    cs_b = Buf("cs")
    bgate = sb("bgate", [128, 48], F32)
    qkg = sb("qkg", [128, 2, 128], F32)
    wdw = sb("wdw", [128, 8, 31], F32)
    cvec = sb("cvec", [128, 5, 8], F32)
    wsT_bf = sb("wsT_bf", [128, 8, 128], BF16)
    Bmat = sb("Bmat", [128, 8, 128], F32)
    const_b = Buf("const")
    lconst_b = Buf("lconst")
    csem = P.dma_sem("const")

    psum = [nc.alloc_psum_tensor("ps%d" % i, [128, 512], F32) for i in range(8)]
    psum_b = [Buf("ps%d" % i) for i in range(8)]
    ps_rr = [0]

    held = set()

    def newps(hold=False):
        for _ in range(17):
            i = ps_rr[0] % 8
            ps_rr[0] += 1
            if i not in held:
                break
        else:
            raise RuntimeError("all PSUM banks held")
        if hold:
            held.add(i)
        return psum[i], psum_b[i]

    def release(*banks):
        for b in banks:
            held.discard(psum_b.index(b[1]))

    def tt(eng, out, in0, in1, op, reads, writes):
        P.op(eng, lambda e: e.tensor_tensor(out=out, in0=in0, in1=in1, op=op), reads, writes)

    def ts(eng, out, in0, s1, s2, op0, op1, reads, writes):
        if op1 is None:
            P.op(eng, lambda e: e.tensor_scalar(out=out, in0=in0, scalar1=s1, scalar2=None, op0=op0), reads, writes)
        else:
            P.op(eng, lambda e: e.tensor_scalar(out=out, in0=in0, scalar1=s1, scalar2=s2, op0=op0, op1=op1), reads, writes)

    def stt(eng, out, in0, scalar, in1, op0, op1, reads, writes):
        P.op(eng, lambda e: e.scalar_tensor_tensor(out=out, in0=in0, scalar=scalar, in1=in1, op0=op0, op1=op1),
             reads, writes)

    def act(out, in_, func, reads, writes, bias=None, scale=None, accum_out=None):
        kw = {}
        if bias is not None:
            kw["bias"] = bias
        if scale is not None:
            kw["scale"] = scale
        if accum_out is not None:
            kw["accum_out"] = accum_out
        P.op("act", lambda e: e.activation(out=out, in_=in_, func=func, **kw), reads, writes)

    def rsqrt(out, in_, scale, eps_col, reads, writes):
        act(out, in_, AF.Sqrt, list(reads) + [const_b], writes, bias=epsc[:, eps_col:eps_col + 1], scale=scale)
        P.op("dve", lambda e: e.reciprocal(out=out, in_=out), writes, writes)

    def cp(eng, out, in_, reads, writes):
        if eng == "act":
            P.op("act", lambda e: e.activation(out=out, in_=in_, func=AF.Copy), reads, writes)
        else:
            P.op(eng, lambda e: e.tensor_copy(out=out, in_=in_), reads, writes)

    def mm(out, lhsT, rhs, start, stop, reads, writes, signal):
        P.op("pe", lambda e: e.matmul(out, lhsT=lhsT, rhs=rhs, start=start, stop=stop), reads, writes, signal=signal)

    def tr(out, in_, reads, writes, signal):
        P.op("pe", lambda e: e.transpose(out, in_, ident[:]), reads, writes, signal=signal)

    def dbg_dump(name, ap, shape, dt, reads):
        if name not in dbg:
            return
        d = nc.dram_tensor("dbg_" + name, list(shape), dt, kind="ExternalOutput").ap()
        dbg_d[name] = d
        sk = P.dma_sem("dbg")
        P.dma("act", sk, lambda e: e.dma_start(out=d, in_=ap), reads=reads, writes=[Buf()])
        P.wait_tok("act", (sk, 16))

    def cload(eng, dst, src, buf):
        P.dma(eng, csem, lambda e: e.dma_start(out=dst, in_=src), reads=[], writes=[buf])

    cload("sp", ident[:], ident_d, const_b)
    P.op("dve", lambda e: e.memset(ones_bf[:], 1.0), [], [const_b])
    P.op("dve", lambda e: e.memset(ones_f[:], 1.0), [], [const_b])
    P.op("dve", lambda e: e.memset(epsc[:, 0:1], EPS_RMS), [], [const_b])
    P.op("dve", lambda e: e.memset(epsc[:, 1:2], EPS_LN), [], [const_b])
    P.op("dve", lambda e: e.memset(junk, 0.0), [], JUNK_B)
    zsem = [P.dma_sem("zst%d" % i) for i in range(2)]
    zl_sem = P.dma_sem("zld")
    for side in range(2):
        c0 = 0 if side == 0 else PADZ + S
        P.dma("act", zsem[side],
              lambda e, c0=c0: e.dma_start(out=zT_d[:, c0:c0 + PADZ].rearrange("(c p) w -> p c w", p=128),
                                           in_=junk[:, 0:8 * PADZ].rearrange("p (c w) -> p c w", c=8)),
              reads=[ob4], writes=[zpad_buf])

    GROUPS = {"in": [(K0, 2560), (Q0, 1024), (SU0, 2048), (G0, 2048), (G0 + 2048, 2048), (G0 + 4096, 2048)]}
    for k_, (r_, c_) in WSHAPE.items():
        if k_ != "in":
            GROUPS[k_] = [(0, c_)]
    grp_buf = {}
    grp_sem = {}
    conv_q = []
    grp_pending = {}

    def plan_convert(l):
        order = [("in", 0), ("in", 1), ("in", 2), ("in", 3), ("in", 4), ("in", 5), ("ao", 0), ("co", 0), ("so", 0),
                 ("out", 0), ("fg", 0), ("fu", 0), ("fd", 0)]
        for k, gi in order:
            col0, ncols = GROUPS[k][gi]
            r, c = WSHAPE[k]
            gkey = (l, k, gi)
            grp_buf[gkey] = Buf("wb_%s%d_%d" % (k, l, gi))
            sk = grp_sem.get((k, gi))
            if sk is None:
                sk = grp_sem[(k, gi)] = P.dma_sem("cv_%s%d" % (k, gi))
            cw = ncols
            a = 1
            while cw > 2048:
                a += 1
                while ncols % a:
                    a += 1
                cw = ncols // a
            if ncols == c:
                src = w_d[k][l].rearrange("r (a c) -> (r a) c", a=a)
                dst = wb_d[(l, k)].rearrange("r (a c) -> (r a) c", a=a)
                rows = r * a
                per = max(16, ((1 << 20) // cw) // 16 * 16)
            else:
                src = w_d[k][l][:, col0:col0 + ncols].rearrange("r (a c) -> r a c", a=a)
                dst = wb_d[(l, k)][:, col0:col0 + ncols].rearrange("r (a c) -> r a c", a=a)
                rows = r
                per = max(16, ((1 << 20) // ncols) // 16 * 16)
            r0 = 0
            n_d = 0
            while r0 < rows:
                n = min(per, rows - r0)

                def emit_one(sk=sk, src=src, dst=dst, r0=r0, n=n, gkey=gkey):
                    P.dma("pool", sk, lambda e: e.dma_start(out=dst[r0:r0 + n], in_=src[r0:r0 + n]),
                          reads=[], writes=[], serialize=False)
                    grp_buf[gkey].last_write = (sk, P.cnt[sk])
                    grp_buf[gkey].readers = {}
                conv_q.append((gkey, emit_one))
                n_d += 1
                r0 += n
            grp_pending[gkey] = n_d

    def pump_one(paced):
        if not conv_q:
            return False
        gkey, fn = conv_q.pop(0)
        if paced and P.cnt["pe"] > 0:
            P.wait_tok("pool", ("pe", P.cnt["pe"]))
        fn()
        grp_pending[gkey] -= 1
        return True

    def ensure_group(gkey):
        while grp_pending[gkey] > 0:
            pump_one(False)

    pump_state = {"rate": 0.0, "credit": 0.0}

    def pump_tick():
        pump_state["credit"] += pump_state["rate"]
        while pump_state["credit"] >= 1.0:
            pump_state["credit"] -= 1.0
            if not pump_one(True):
                pump_state["credit"] = 0.0
                break

    def group_of(l, k, col0):
        for gi, (c0, nc_) in enumerate(GROUPS[k]):
            if c0 <= col0 < c0 + nc_:
                return (l, k, gi)
        raise KeyError((k, col0))

    slab_ctr = [0]

    def load_sub(l, k, krow0, nk, col0):
        i = slab_ctr[0] % NSLOT
        slab_ctr[0] += 1
        src = wb_d[(l, k)][krow0 * 128:(krow0 + nk) * 128, col0:col0 + 512].rearrange("(kc p) n -> p kc n", p=128)
        gkey = group_of(l, k, col0)
        ensure_group(gkey)
        P.dma("sp", ring_sem[i], lambda e: e.dma_start(out=ring[i][:, 0:nk, :], in_=src),
              reads=[grp_buf[gkey]], writes=[ring_b[i]])
        return ring[i], ring_b[i]

    def linear(l, k, col0, nkc, krow0, rhs_fn, mode, out_fn, extra_reads, banks=None):
        own = banks is None
        if own:
            banks = [newps(hold=True) for _ in range(4)]
        nsub = (nkc + KSUB - 1) // KSUB
        for ss in range(nsub):
            k0 = ss * KSUB
            nk = min(KSUB, nkc - k0)
            slab, slab_b = load_sub(l, k, krow0 + k0, nk, col0)
            for m in range(4):
                ps, psb = banks[m]
                for kk in range(nk):
                    kc = k0 + kk
                    first = own and kc == 0
                    last = own and kc == nkc - 1
                    if mode == "F":
                        mm(ps[:], slab[:, kk, m * 128:(m + 1) * 128], rhs_fn(kc), first, last,
                           [slab_b] + extra_reads, [psb], signal=(kk == nk - 1))
                    else:
                        mm(ps[:], rhs_fn(kc, m), slab[:, kk, :], first, last,
                           [slab_b] + extra_reads, [psb], signal=(kk == nk - 1))
            pump_tick()
        if own and out_fn is not None:
            for m in range(4):
                out_fn(m, banks[m][0], banks[m][1])
                release(banks[m])
        return banks

    gsem = P.dma_sem("grep")

    def norm_to_hT(g_src, src_fn=None):
        if src_fn is None:
            src_fn = lambda s: (xt[:, s, :], [xt_b[s]])
        P.dma("act", gsem, lambda e: e.dma_start(out=g_rep, in_=g_src), reads=[], writes=G_REP_B)
        for s in range(4):
            sap, sbufs = src_fn(s)
            act(junk, sap, AF.Square, sbufs, JUNK_B + [small_b], accum_out=small[:, s:s + 1])
        rsqrt(small[:, 8:12], small[:, 0:4], 1.0 / D, 0, [small_b], [small_b])
        for s in range(4):
            sap, sbufs = src_fn(s)
            stt("dve", xs_t, sap, small[:, 8 + s:9 + s], g_rep, ALU.mult, ALU.mult,
                sbufs + [small_b] + G_REP_B, XS_B)
            for half in range(2):
                ps, psb = newps()
                pv = ps[:].bitcast(BF16)
                for c in range(8):
                    cc = half * 8 + c
                    tr(pv[:, c * 128:(c + 1) * 128], xs_t[:, cc * 128:(cc + 1) * 128], XS_B + [const_b], [psb],
                       signal=(c == 7))
                eng = "act" if half == 0 else "dve"
                cp(eng, hT[:, half * 8:half * 8 + 8, s * 128:(s + 1) * 128],
                   pv.rearrange("p (c t) -> p c t", c=8), [psb], [hT_b])

    def qk_norm_rope(ps, psb, H, gi, s, outb, outb_b):
        W = H * 128
        sq = tmp(0)
        act(sq[:, 0:W], ps[:, 0:W], AF.Square, [psb], [ov2_b[0]])
        P.op("dve", lambda e: e.tensor_reduce(out=small[:, 16:16 + H], in_=sq[:, 0:W].rearrange("p (h d) -> p h d", h=H),
                                              axis=AX.X, op=ALU.add), [ov2_b[0], small_b], [small_b])
        rsqrt(small[:, 24:24 + H], small[:, 16:16 + H], 1.0 / 128, 0, [small_b], [small_b])
        qn = tmp(1)
        qn3 = qn[:, 0:W].rearrange("p (h d) -> p h d", h=H)
        tt("dve", qn3, ps[:, 0:W].rearrange("p (h d) -> p h d", h=H),
           small[:, 24:24 + H].unsqueeze(2).to_broadcast([128, H, 128]), ALU.mult, [psb, small_b], [ov2_b[1]])
        tt("dve", qn3, qn3, qkg[:, gi, :].unsqueeze(1).to_broadcast([128, H, 128]), ALU.mult,
           [ov2_b[1], lconst_b], [ov2_b[1]])
        cosb = cs_t[:, 0, s, :].unsqueeze(1).to_broadcast([128, H, 64])
        sinb = cs_t[:, 1, s, :].unsqueeze(1).to_broadcast([128, H, 64])
        x1 = qn3[:, :, 0:64]
        x2 = qn3[:, :, 64:128]
        o3 = outb[:, 0:W].rearrange("p (h d) -> p h d", h=H)
        t1 = tmp(2)[:, 0:H * 64].rearrange("p (h d) -> p h d", h=H)
        t2 = tmp(2)[:, 256:256 + H * 64].rearrange("p (h d) -> p h d", h=H)
        t3 = tmp(3)[:, 0:H * 64].rearrange("p (h d) -> p h d", h=H)
        t4 = tmp(3)[:, 256:256 + H * 64].rearrange("p (h d) -> p h d", h=H)
        tt("dve", t1, x1, cosb, ALU.mult, [ov2_b[1], cs_b], [ov2_b[2]])
        tt("dve", t2, x2, sinb, ALU.mult, [ov2_b[1], cs_b], [ov2_b[2]])
        tt("dve", o3[:, :, 0:64], t1, t2, ALU.subtract, [ov2_b[2]], [outb_b])
        tt("pool", t3, x2, cosb, ALU.mult, [ov2_b[1], cs_b], [ov2_b[3]])
        tt("pool", t4, x1, sinb, ALU.mult, [ov2_b[1], cs_b], [ov2_b[3]])
        tt("pool", o3[:, :, 64:128], t3, t4, ALU.add, [ov2_b[3]], [outb_b])

    xl_sem = P.dma_sem("xld")
    xst_sem = P.dma_sem("xst")
    cs_sem = P.dma_sem("cs")
    lc_sem = P.dma_sem("lconst")

    pn_sem = [P.dma_sem("pn%d" % i) for i in range(2)]
    xsA = mgT[:].rearrange("p a b -> p (a b)").bitcast(F32).rearrange("p (s d) -> p s d", s=2)
    xsB = stg[:].rearrange("p a b -> p (a b)").rearrange("p (s d) -> p s d", s=2)

    def prenorm(src_ap, src_bufs, t1, g_src):
        P.dma("sp", pn_sem[0],
              lambda e: e.dma_start(out=xsA, in_=src_ap[t1 * T:t1 * T + 256, :].rearrange("(s p) d -> p s d", p=128)),
              reads=src_bufs, writes=mg_b)
        P.dma("sp", pn_sem[1],
              lambda e: e.dma_start(out=xsB, in_=src_ap[t1 * T + 256:t1 * T + 512, :].rearrange("(s p) d -> p s d", p=128)),
              reads=src_bufs, writes=stg_b)
        norm_to_hT(g_src, lambda s: ((xsA[:, s, :], list(mg_b)) if s < 2 else (xsB[:, s - 2, :], list(stg_b))))

    def load_x_tile(src_ap, src_bufs, t, q="act"):
        P.dma(q, xl_sem,
              lambda e: e.dma_start(out=xt[:], in_=src_ap[t * T:(t + 1) * T, :].rearrange("(s p) d -> p s d", p=128)),
              reads=src_bufs, writes=xt_b)

    def load_cs(t):
        P.dma("act", cs_sem, lambda e: e.dma_start(out=cs_t[:, 0, :, :], in_=cos_d[:, t * 4:(t + 1) * 4, :]),
              reads=[], writes=[cs_b])
        P.dma("act", cs_sem, lambda e: e.dma_start(out=cs_t[:, 1, :, :], in_=sin_d[:, t * 4:(t + 1) * 4, :]),
              reads=[], writes=[cs_b])

    plan_convert(0)
    pump_state["rate"] = 0.85

    for l in range(L):
        x_src = x_d if l == 0 else xs_d[l - 1]
        x_src_b = (lambda t: []) if l == 0 else (lambda t, l=l: [xs_buf[l - 1][t]])
        last_layer = (l == L - 1)

        def lload(dst, src):
            P.dma("act", lc_sem, lambda e: e.dma_start(out=dst, in_=src), reads=[], writes=[lconst_b])
        lload(bgate[:], bgate_d[l])
        lload(qkg[:, 0, :], qg_d[l])
        lload(qkg[:, 1, :], kg_d[l])
        lload(wdw[:], wdw_d[l])
        lload(cvec[:], cvec_d[l])
        P.dma("act", lc_sem, lambda e, l=l: e.dma_start(out=stg[:, 0:2, :].rearrange("p a t -> p (a t)").rearrange("p (g q) -> p g q", g=8), in_=wsT_d[l]),
              reads=[], writes=stg_b[0:4] + [lconst_b])
        P.dma("act", lc_sem, lambda e, l=l: e.dma_start(out=stg[:, 2:4, :].rearrange("p a t -> p (a t)").rearrange("p (g q) -> p g q", g=8), in_=bs_d[l]),
              reads=[], writes=stg_b[0:4] + [lconst_b])
        wsf = stg[:, 0:2, :].rearrange("p a t -> p (a t)").rearrange("p (g q) -> p g q", g=8)
        bsr = stg[:, 2:4, :].rearrange("p a t -> p (a t)").rearrange("p (g q) -> p g q", g=8)
        cp("dve", wsT_bf[:], wsf, [lconst_b] + stg_b[0:4], [lconst_b])
        for gq in range(2):
            ps, psb = newps()
            for g4 in range(4):
                g = gq * 4 + g4
                mm(ps[:, g4 * 128:(g4 + 1) * 128], ones_f[:], wsf[:, g, :], True, True, [const_b, lconst_b] + stg_b[0:4],
                   [psb], signal=(g4 == 3))
            for g4 in range(4):
                g = gq * 4 + g4
                stt("dve", Bmat[:, g, :], ps[:, g4 * 128:(g4 + 1) * 128], cvec[:, 4, g:g + 1], bsr[:, g, :],
                    ALU.mult, ALU.add, [psb, lconst_b] + stg_b[0:4], [lconst_b])

        if l == 0:
            dbg_dump("Bmat", Bmat[:], [128, 8, 128], F32, [lconst_b])
            dbg_dump("wsTbf", wsT_bf[:], [128, 8, 128], BF16, [lconst_b])
            dbg_dump("cvec", cvec[:], [128, 5, 8], F32, [lconst_b])
        for t in range(NT):
            load_x_tile(x_src, x_src_b(t), t)
            load_cs(t)
            norm_to_hT(gmix_d[l])
            if l == 0 and t == 0:
                dbg_dump("hT", hT[:], [128, 16, T], BF16, [hT_b])

            def kv_out(s, ps, psb, t=t):
                ob, obb = qnb[s % 2], qnb_b[s % 2]
                qk_norm_rope(ps, psb, 2, 1, s, ob, obb)
                cp("act", Vt[:, t * 4 + s, :], ps[:, 256:512], [psb], [Vt_b])
                p2, p2b = newps()
                pv = p2[:].bitcast(BF16)
                for h in range(2):
                    tr(pv[:, h * 128:(h + 1) * 128], ob[:, h * 128:(h + 1) * 128], [obb, const_b], [p2b], signal=(h == 1))
                cp("act", kT[:, :, t * T + s * 128:t * T + (s + 1) * 128],
                   pv[:, 0:256].rearrange("p (h t) -> p h t", h=2), [p2b], [kT_b])
            linear(l, "in", K0, 16, 0, lambda kc, s: hT[:, kc, s * 128:(s + 1) * 128], "T", kv_out, [hT_b])

            for j in range(2):
                def g_out(m, ps, psb):
                    act(stg[:, m, :], ps[:], AF.Sigmoid, [psb], [stg_b[m]])
                linear(l, "in", CG0 + 512 * j, 16, 0, lambda kc: hT[:, kc, :], "F", g_out, [hT_b])

                def a_out(m, ps, psb, j=j):
                    tt("dve", brT[:, m, :], ps[:], stg[:, m, :], ALU.mult, [psb, stg_b[m]], [br_b[m]])
                linear(l, "in", CA0 + 512 * j, 16, 0, lambda kc: hT[:, kc, :], "F", a_out, [hT_b])
                c0 = PADZ + t * T
                P.dma("act", zsem[j],
                      lambda e, j=j, c0=c0: e.dma_start(
                          out=zT_d[j * 512:(j + 1) * 512, c0:c0 + T].rearrange("(c p) w -> p c w", p=128),
                          in_=brT[:, 0:4, :]),
                      reads=br_b[0:4], writes=[z_buf[t][j]])
        if l == 0:
            dbg_dump("kT", kT[:], [128, 2, S], BF16, [kT_b])
            dbg_dump("Vt", Vt[:], [128, 16, 256], BF16, [Vt_b])

        def finalize_tile(tp, l=l, last_layer=last_layer):
            if not last_layer:
                P.dma("sp", xst_sem,
                      lambda e: e.dma_start(out=xs_d[l][tp * T:(tp + 1) * T, :].rearrange("(s p) d -> p s d", p=128),
                                            in_=xt[:]),
                      reads=xt_b, writes=xs_buf[l][tp] if isinstance(xs_buf[l][tp], list) else [xs_buf[l][tp]])
            else:
                P.dma("act", gsem, lambda e: e.dma_start(out=g_rep, in_=gfin_d), reads=[], writes=G_REP_B)
                for s in range(4):
                    act(junk, xt[:, s, :], AF.Square, [xt_b[s]], JUNK_B + [small_b], accum_out=small[:, s:s + 1])
                rsqrt(small[:, 8:12], small[:, 0:4], 1.0 / D, 0, [small_b], [small_b])
                for s in range(4):
                    stt("dve", xt[:, s, :], xt[:, s, :], small[:, 8 + s:9 + s], g_rep, ALU.mult, ALU.mult,
                        [xt_b[s], small_b] + G_REP_B, [xt_b[s]])
                P.dma("sp", xst_sem,
                      lambda e: e.dma_start(out=out_d[tp * T:(tp + 1) * T, :].rearrange("(s p) d -> p s d", p=128),
                                            in_=xt[:]),
                      reads=xt_b, writes=[Buf()])

        for t in range(ntilesB):
            if l == 0 and t == 0:
                pump_state["rate"] = 0.75
            if l == 0 and t == 1 and L > 1:
                while pump_one(True):
                    pass
                plan_convert(1)
                pump_state["rate"] = 0.2
            load_cs(t)
            if t == 0:
                load_x_tile(x_src, x_src_b(t), t)
                norm_to_hT(gmix_d[l])

            for j in range(2):
                def q_out(s, ps, psb, j=j):
                    ob, obb = qnb[s % 2], qnb_b[s % 2]
                    qk_norm_rope(ps, psb, 4, 0, s, ob, obb)
                    p2, p2b = newps()
                    pv = p2[:].bitcast(BF16)
                    for h in range(4):
                        tr(pv[:, h * 128:(h + 1) * 128], ob[:, h * 128:(h + 1) * 128], [obb, const_b], [p2b],
                           signal=(h == 3))
                    cp("act", mgT[:, 4 * j:4 * j + 4, s * 128:(s + 1) * 128],
                       pv[:, 0:512].rearrange("p (h t) -> p h t", h=4), [p2b], mg_b[4 * j:4 * j + 4])
                linear(l, "in", Q0 + 512 * j, 16, 0, lambda kc, s: hT[:, kc, s * 128:(s + 1) * 128], "T", q_out, [hT_b])
            if t > 0:
                finalize_tile(t - 1)
                load_x_tile(x_src, x_src_b(t), t, q="sp")
            if l == 0 and t == 0:
                dbg_dump("qT", mgT[:, 0:8, :], [128, 8, T], BF16, mg_b[0:8])

            LOOK = 3
            NIT = 8 * 16

            def emit_scores(i):
                h, kc = divmod(i, 16)
                kvh = h // 4
                ps, psb = newps()
                mm(ps[:], kT[:, kvh, kc * 128:(kc + 1) * 128], mgT[:, h, :], True, True, [kT_b, mg_b[h]], [psb], True)
                ptt, pttb = pt[i % 4], pt_b[i % 4]
                act(ptt[:], ps[:], AF.Exp, [psb], [pttb], scale=SM_SCALE)

            for i in range(LOOK):
                emit_scores(i)
            for i in range(NIT):
                h, kc = divmod(i, 16)
                kvh = h // 4
                if kc == 0:
                    pso, psob = newps(hold=True)
                    pss, pssb = newps(hold=True)
                ptt, pttb = pt[i % 4], pt_b[i % 4]
                mm(pso[:], Vt[:, kc, kvh * 128:(kvh + 1) * 128], ptt[:], kc == 0, kc == 15, [Vt_b, pttb], [psob], False)
                mm(pss[:], ones_bf[:], ptt[:], kc == 0, kc == 15, [const_b, pttb], [pssb], True)
                if i + LOOK < NIT:
                    emit_scores(i + LOOK)
                if kc == 15:
                    P.op("dve", lambda e, pss=pss: e.reciprocal(out=tmp(5), in_=pss[:]), [pssb], [ov2_b[5]])
                    tt("dve", brT[:, h, :], pso[:], tmp(5), ALU.mult, [psob, ov2_b[5]], [br_b[h]])
                    release((pso, psob), (pss, pssb))
            if l == 0 and t == 0:
                dbg_dump("attnT", brT[:, 0:8, :], [128, 8, T], BF16, br_b[0:8])

            tl = [z_buf[tt_][jj] for tt_ in range(max(0, t - 1), min(NT, t + 2)) for jj in range(2)]
            P.dma("act", zl_sem,
                  lambda e, t=t: e.dma_start(out=zw[:, :, 0:T + 2 * PADZ],
                                             in_=zT_d[:, t * T:t * T + T + 2 * PADZ].rearrange("(c p) w -> p c w", p=128)),
                  reads=tl + [zpad_buf], writes=ZW_B)
            psm, psmb = newps(hold=True)
            psq, psqb = newps(hold=True)

            def conv_chunk(c, psm=psm, psmb=psmb, psq=psq, psqb=psqb):
                P.op("pool", lambda e: e.tensor_tensor(
                    out=diag[:, 0:16, :], in0=ident[:].unsqueeze(1).to_broadcast([128, 16, 128]),
                    in1=wdw[:, c, 0:16].unsqueeze(2).to_broadcast([128, 16, 128]), op=ALU.mult),
                    [const_b, lconst_b], DLO_B)
                P.op("pool", lambda e: e.tensor_tensor(
                    out=diag[:, 16:31, :], in0=ident[:].unsqueeze(1).to_broadcast([128, 15, 128]),
                    in1=wdw[:, c, 16:31].unsqueeze(2).to_broadcast([128, 15, 128]), op=ALU.mult),
                    [const_b, lconst_b], DHI_B)
                ps, psb = newps()
                for kk in range(31):
                    mm(ps[:], diag[:, kk, :], zw[:, c, kk:kk + T], kk == 0, kk == 30,
                       ZW_B + (DLO_B if kk < 16 else DHI_B), [psb], kk in (15, 30))
                act(stg[:, c, :], ps[:], AF.Identity, [psb, lconst_b], [stg_b[c]], bias=cvec[:, 0, c:c + 1])
                sqb, sqbb = pt[c % 4], pt_b[c % 4]
                act(sqb[:], stg[:, c, :], AF.Square, [stg_b[c]], [sqbb])
                mm(psm[:], ones_f[:], stg[:, c, :], c == 0, c == 7, [const_b, stg_b[c]], [psmb], True)
                mm(psq[:], ones_bf[:], sqb[:], c == 0, c == 7, [const_b, sqbb], [psqb], True)

            def conv_finish(psm=psm, psmb=psmb, psq=psq, psqb=psqb):
                mean = tmp(0)
                rstd = tmp(1)
                ts("dve", mean, psm[:], 1.0 / 1024, None, ALU.mult, None, [psmb], [ov2_b[0]])
                tt("dve", tmp(2), mean, mean, ALU.mult, [ov2_b[0]], [ov2_b[2]])
                stt("dve", rstd, psq[:], 1.0 / 1024, tmp(2), ALU.mult, ALU.subtract, [psqb, ov2_b[2]], [ov2_b[1]])
                rsqrt(rstd, rstd, 1.0, 1, [ov2_b[1]], [ov2_b[1]])
                release((psm, psmb), (psq, psqb))
                for c in range(8):
                    u = pr[c % 2]
                    ub = pr_b[c % 2]
                    tt("dve", u[:], stg[:, c, :], mean, ALU.subtract, [stg_b[c], ov2_b[0]], [ub])
                    tt("dve", u[:], u[:], rstd, ALU.mult, [ub, ov2_b[1]], [ub])
                    act(brT[:, 8 + c, :], u[:], AF.Silu, [ub, lconst_b], [br_b[8 + c]],
                        bias=cvec[:, 2, c:c + 1], scale=cvec[:, 1, c:c + 1])

            sgp = [gt[0], gt[1], pr[0], pr[1]]
            sgp_b = [gt_b[0], gt_b[1], pr_b[0], pr_b[1]]

            def gelu(dst, src_ps, psb, dstb, k1, k2):
                a1, a2 = tmp(k1), tmp(k2)
                act(a1, src_ps, AF.Square, [psb], [ov2_b[k1]])
                ts("dve", a1, a1, GELU_C, 1.0, ALU.mult, ALU.add, [ov2_b[k1]], [ov2_b[k1]])
                tt("dve", a1, a1, src_ps, ALU.mult, [ov2_b[k1], psb], [ov2_b[k1]])
                act(a2, a1, AF.Sigmoid, [ov2_b[k1]], [ov2_b[k2]], scale=GELU_S)
                tt("dve", dst, a2, src_ps, ALU.mult, [ov2_b[k2], psb], [dstb])

            for j in range(2):
                def v_out(s, ps, psb, j=j):
                    gv = tmp(4)
                    gelu(gv, ps[:], psb, ov2_b[4], 0, 1)
                    gv3 = gv.rearrange("p (g c) -> p g c", g=4)
                    P.op("dve", lambda e: e.tensor_reduce(out=small[:, 32:36], in_=gv3, axis=AX.X, op=ALU.add),
                         [ov2_b[4], small_b], [small_b])
                    act(tmp(5), gv, AF.Square, [ov2_b[4]], [ov2_b[5]])
                    P.op("dve", lambda e: e.tensor_reduce(out=small[:, 36:40],
                                                          in_=tmp(5).rearrange("p (g c) -> p g c", g=4), axis=AX.X,
                                                          op=ALU.add), [ov2_b[5], small_b], [small_b])
                    ts("dve", small[:, 32:36], small[:, 32:36], 1.0 / 128, None, ALU.mult, None, [small_b], [small_b])
                    tt("dve", small[:, 40:44], small[:, 32:36], small[:, 32:36], ALU.mult, [small_b], [small_b])
                    stt("dve", small[:, 36:40], small[:, 36:40], 1.0 / 128, small[:, 40:44], ALU.mult, ALU.subtract,
                        [small_b], [small_b])
                    rsqrt(small[:, 36:40], small[:, 36:40], 1.0, 1, [small_b], [small_b])
                    tt("dve", gv3, gv3, small[:, 32:36].unsqueeze(2).to_broadcast([128, 4, 128]), ALU.subtract,
                       [ov2_b[4], small_b], [ov2_b[4]])
                    vb, vbb = qnb[s % 2], qnb_b[s % 2]
                    tt("dve", vb[:].rearrange("p (g c) -> p g c", g=4), gv3,
                       small[:, 36:40].unsqueeze(2).to_broadcast([128, 4, 128]), ALU.mult, [ov2_b[4], small_b], [vbb])
                    conv_chunk(4 * j + s)
                    sp, spb_ = newps()
                    for g4 in range(4):
                        g = 4 * j + g4
                        mm(sp[:, g4 * 128:(g4 + 1) * 128], vb[:, g4 * 128:(g4 + 1) * 128], wsT_bf[:, g, :], True, True,
                           [vbb, lconst_b], [spb_], g4 == 3)
                    for g4 in range(4):
                        g = 4 * j + g4
                        stt("dve", sgp[g4][:, s * 128:(s + 1) * 128], sp[:, g4 * 128:(g4 + 1) * 128], cvec[:, 3, g:g + 1],
                            Bmat[:, g, :], ALU.mult, ALU.add, [spb_, lconst_b], [sgp_b[g4]])
                linear(l, "in", SV0 + 512 * j, 16, 0, lambda kc, s: hT[:, kc, s * 128:(s + 1) * 128], "T", v_out, [hT_b])

                def u_out(m, ps, psb, j=j):
                    ug = tmp(4)
                    gelu(ug, ps[:], psb, ov2_b[4], 2, 3)
                    tt("dve", brT[:, 16 + 4 * j + m, :], ug, sgp[m][:], ALU.mult, [ov2_b[4], sgp_b[m]],
                       [br_b[16 + 4 * j + m]])
                linear(l, "in", SU0 + 512 * j, 16, 0, lambda kc: hT[:, kc, :], "F", u_out, [hT_b])
            conv_finish()
            if l == 0 and t == 0:
                dbg_dump("convT", brT[:, 8:16, :], [128, 8, T], BF16, br_b[8:16])
                dbg_dump("sgT", brT[:, 16:24, :], [128, 8, T], BF16, br_b[16:24])


            for j in range(4):
                for i, bk in enumerate(("ao", "co", "so")):
                    def gate_out(m, ps, psb, i=i, j=j):
                        act(gt[m % 2][:], ps[:], AF.Sigmoid, [psb, lconst_b], [gt_b[m % 2]],
                            bias=bgate[:, i * 16 + j * 4 + m:i * 16 + j * 4 + m + 1])
                    gbanks = linear(l, "in", G0 + i * 2048 + j * 512, 16, 0, lambda kc: hT[:, kc, :], "F",
                                    None, [hT_b])
                    ybanks = linear(l, bk, j * 512, 8, 0, lambda kc, i=i: brT[:, 8 * i + kc, :], "F",
                                    None, br_b[8 * i:8 * i + 8])
                    for m in range(4):
                        gate_out(m, gbanks[m][0], gbanks[m][1])
                        yps, ypsb = ybanks[m]
                        if i == 0:
                            tt("dve", stg[:, 4 + m, :], gt[m % 2][:], yps[:], ALU.mult, [gt_b[m % 2], ypsb], [stg_b[4 + m]])
                        else:
                            prod, prodb = pr[m % 2], pr_b[m % 2]
                            tt("dve", prod[:], gt[m % 2][:], yps[:], ALU.mult, [gt_b[m % 2], ypsb], [prodb])
                            if i == 1:
                                tt("dve", stg[:, 4 + m, :], stg[:, 4 + m, :], prod[:], ALU.add, [stg_b[4 + m], prodb],
                                   [stg_b[4 + m]])
                            else:
                                tt("dve", mgT[:, 4 * j + m, :], stg[:, 4 + m, :], prod[:], ALU.add,
                                   [stg_b[4 + m], prodb], [mg_b[4 * j + m]])
                    release(*gbanks)
                    release(*ybanks)
            if l == 0 and t == 0:
                dbg_dump("mgT", mgT[:], [128, 16, T], BF16, mg_b)

            for j in range(4):
                def o_out(s, ps, psb, j=j):
                    tt("dve", xt[:, s, j * 512:(j + 1) * 512], xt[:, s, j * 512:(j + 1) * 512], ps[:], ALU.add,
                       [xt_b[s], psb], [xt_b[s]])
                linear(l, "out", j * 512, 16, 0, lambda kc, s: mgT[:, kc, s * 128:(s + 1) * 128], "T", o_out, mg_b)
            if l == 0 and t == 0:
                dbg_dump("xmix", xt[:], [128, 4, D], F32, xt_b)

            norm_to_hT(gffn_d[l])
            for (j0, j1) in ((0, 6), (6, 11)):
                for j in range(j0, j1):
                    gb = linear(l, "fg", j * 512, 16, 0, lambda kc: hT[:, kc, :], "F", None, [hT_b])
                    ub_ = linear(l, "fu", j * 512, 16, 0, lambda kc: hT[:, kc, :], "F", None, [hT_b])
                    for m in range(4):
                        act(gt[m % 2][:], gb[m][0][:], AF.Silu, [gb[m][1]], [gt_b[m % 2]])
                        ci = (j - j0) * 4 + m
                        tt("dve", brT[:, ci, :], gt[m % 2][:], ub_[m][0][:], ALU.mult, [gt_b[m % 2], ub_[m][1]], [br_b[ci]])
                    release(*gb)
                    release(*ub_)
                if j0 == 6 and t + 1 < ntilesB:
                    prenorm(x_src, x_src_b(t + 1), t + 1, gmix_d[l])
                nkc = (j1 - j0) * 4
                for jo in range(4):
                    def d_out(s, ps, psb, jo=jo):
                        tt("dve", xt[:, s, jo * 512:(jo + 1) * 512], xt[:, s, jo * 512:(jo + 1) * 512], ps[:], ALU.add,
                           [xt_b[s], psb], [xt_b[s]])
                    linear(l, "fd", jo * 512, nkc, j0 * 4, lambda kc, s: brT[:, kc, s * 128:(s + 1) * 128], "T", d_out,
                           br_b[0:nkc])

            if t == ntilesB - 1:
                finalize_tile(t)

    for key in (xst_sem, zsem[0], zsem[1]):
        if P.cnt[key] > 0:
            P.wait_tok("act", (key, P.cnt[key]))
    P.emit()
    return nc, dbg_d


def _rope_tables():
    rows = S // 64
    row = np.repeat(np.arange(rows, dtype=np.float32), 64)
    col = np.tile(np.arange(64, dtype=np.float32), rows)
    inv = (np.float32(10000.0) ** (-np.arange(32, dtype=np.float32) / np.float32(32))).astype(np.float32)
    ang = np.concatenate([row[:, None] * inv, col[:, None] * inv], axis=-1).astype(np.float32)
    cos = np.cos(ang).astype(np.float32)
    sin = np.sin(ang).astype(np.float32)
    return (np.ascontiguousarray(cos.reshape(16, 128, 64).transpose(1, 0, 2)),
            np.ascontiguousarray(sin.reshape(16, 128, 64).transpose(1, 0, 2)))


def _prep_shared(inp, L):
    f = lambda a: np.ascontiguousarray(np.asarray(a, dtype=np.float32))
    sh = {}
    sh["w_in"] = f(inp["w_in"][:L])
    sh["w_attn_o"] = f(inp["w_attn_o"][:L])
    sh["w_conv_o"] = f(inp["w_conv_o"][:L])
    sh["w_sg_o"] = f(inp["w_sg_o"][:L])
    sh["w_out"] = f(inp["w_out"][:L])
    sh["w_ff_gate"] = f(inp["w_ff_gate"][:L])
    sh["w_ff_up"] = f(inp["w_ff_up"][:L])
    sh["w_ff_down"] = f(inp["w_ff_down"][:L])
    rep = lambda v: np.ascontiguousarray(np.broadcast_to(np.asarray(v, np.float32)[..., None, :],
                                                         v.shape[:-1] + (128, v.shape[-1])))
    sh["g_mix_rep"] = rep(np.asarray(inp["g_mix"])[:L])
    sh["g_ffn_rep"] = rep(np.asarray(inp["g_ffn"])[:L])
    sh["g_final_rep"] = rep(np.asarray(inp["g_final"]))
    colT = lambda v, n: np.ascontiguousarray(np.asarray(v, np.float32).reshape(v.shape[0], n, 128).transpose(0, 2, 1))
    sh["b_gate_T"] = colT(np.asarray(inp["b_gate"])[:L], 48)
    sh["qg_rep"] = rep(np.asarray(inp["q_norm_g"])[:L])
    sh["kg_rep"] = rep(np.asarray(inp["k_norm_g"])[:L])
    wdw = np.asarray(inp["w_dw"], np.float32)[:L, :, 0, :]
    sh["wdw_T"] = np.ascontiguousarray(wdw.reshape(L, 31, 8, 128).transpose(0, 3, 2, 1))
    vecs = [inp["b_dw"], inp["conv_ln_g"], inp["conv_ln_b"], inp["sg_ln_g"], inp["sg_ln_b"]]
    sh["cvec_T"] = np.ascontiguousarray(np.stack([colT(np.asarray(v)[:L], 8) for v in vecs], axis=2))
    ws = np.asarray(inp["w_s"], np.float32)[:L]
    sh["wsT"] = np.ascontiguousarray(ws.transpose(0, 3, 1, 2))
    bs = np.asarray(inp["b_s"], np.float32)[:L]
    sh["bs_rep"] = np.ascontiguousarray(np.broadcast_to(bs[:, None, :, :], (L, 128, 8, 128)))
    c, s = _rope_tables()
    sh["cos_t"] = c
    sh["sin_t"] = s
    sh["ident_bf"] = np.eye(128, dtype=np.float32).astype(ml_dtypes.bfloat16)
    return sh


_CACHE = {}


def kernel(**inputs):
    L = DEPTH
    if "nc" not in _CACHE:
        _CACHE["nc"] = build(L)[0]
    nc = _CACHE["nc"]
    sh = _prep_shared(inputs, L)
    x = np.asarray(inputs["x"], dtype=np.float32)
    in_maps = []
    for b in range(NB):
        m = dict(sh)
        m["x"] = np.ascontiguousarray(x[b])
        in_maps.append(m)
    res = run_bass_kernel_spmd(nc, in_maps, core_ids=list(range(NB)))
    out = np.stack([np.asarray(res.results[b]["out"], dtype=np.float32) for b in range(NB)], axis=0)
    return out
```
